# Optimizing a Trainium2 kernel written in Bass

```python
import math
import jax
import jax.numpy as jnp
from jax import lax
import numpy as np

D_MODEL = 1024
BATCH = 8
SEQ = 4096
DEPTH = 1

MEM_LEN = 256
MOBA_HEADS = 8
MOBA_HEAD_DIM = 64
MOBA_BLOCK = 256
MOBA_TOPK = 3
MOBA_Q_CHUNK = 32
RET_HEADS = 4
RET_QK_DIM = 128
RET_V_DIM = 128
RET_CHUNK = 128
ROPE_BASE = 10000.0
MEM_HEADS = 4
MEM_HEAD_DIM = 128
REL_BUCKETS = 32
REL_MAX_DIST = 2048
D_FF = 2816
CONV_WIDTH = 3
EPS = 1e-6
NEG_INF = -1e30

MOBA_W = MOBA_HEADS * MOBA_HEAD_DIM
RET_QK_W = RET_HEADS * RET_QK_DIM
RET_V_W = RET_HEADS * RET_V_DIM
MEM_W = MEM_HEADS * MEM_HEAD_DIM
IN_SIZES = (MOBA_W, MOBA_W, MOBA_W, RET_QK_W, RET_QK_W, RET_V_W, RET_V_W, MEM_W, D_MODEL, D_MODEL, D_MODEL)
IN_SPLIT_AT = tuple(int(s) for s in np.cumsum(IN_SIZES)[:-1])
D_IN = int(sum(IN_SIZES))

kernel_name = "hybrid_moba_retention_memory_block"


def rmsnorm(x, g):
    xf = x.astype(jnp.float32)
    y = xf * lax.rsqrt(jnp.mean(xf * xf, axis=-1, keepdims=True) + EPS)
    return (y * g.astype(jnp.float32)).astype(x.dtype)


def rel_bucket(dist):
    max_exact = REL_BUCKETS // 2
    d = jnp.maximum(dist, 0)
    df = jnp.maximum(d, 1).astype(jnp.float32)
    large = max_exact + (jnp.log(df / max_exact) / math.log(REL_MAX_DIST / max_exact)
                         * (REL_BUCKETS - max_exact)).astype(jnp.int32)
    large = jnp.minimum(large, REL_BUCKETS - 1)
    return jnp.where(d < max_exact, d, large)


def rotary(x, pos):
    half = x.shape[-1] // 2
    inv = ROPE_BASE ** (-jnp.arange(half, dtype=jnp.float32) / half)
    ang = pos.astype(jnp.float32)[:, None] * inv
    cos = jnp.cos(ang)[None, :, None, :]
    sin = jnp.sin(ang)[None, :, None, :]
    x1, x2 = x[..., :half], x[..., half:]
    return jnp.concatenate([x1 * cos - x2 * sin, x1 * sin + x2 * cos], axis=-1).astype(x.dtype)


def moba_attention(q, k, v, rel_bias):
    B, S, H, dh = q.shape
    nb = -(-S // MOBA_BLOCK)
    s_pad = nb * MOBA_BLOCK
    n_sel = min(MOBA_TOPK, nb)
    qc_len = MOBA_Q_CHUNK
    scale = dh ** -0.5
    q = q.transpose(0, 2, 1, 3)
    pad = ((0, 0), (0, 0), (0, s_pad - S), (0, 0))
    k = jnp.pad(k.transpose(0, 2, 1, 3), pad)
    v = jnp.pad(v.transpose(0, 2, 1, 3), pad)
    kb = k.reshape(B, H, nb, MOBA_BLOCK, dh)
    vb = v.reshape(B, H, nb, MOBA_BLOCK, dh)
    k_mean = jnp.mean(kb.astype(jnp.float32), axis=3)

    pos = jnp.arange(S)
    fully_past = jnp.arange(nb)[None, :] < (pos // MOBA_BLOCK)[:, None]
    gate = jnp.einsum('bhsd,bhnd->bhsn', q.astype(jnp.float32), k_mean)
    gate = jnp.where(fully_past, gate, NEG_INF)
    _, sel = lax.top_k(gate, n_sel)

    table = rel_bias.T.astype(jnp.float32)
    b_idx = jnp.arange(B)[:, None, None, None]
    h_idx = jnp.arange(H)[None, :, None, None]
    blk_off = jnp.arange(MOBA_BLOCK)

    def query_chunk(ci):
        c0 = ci * qc_len
        q_pos = c0 + jnp.arange(qc_len)
        blk = c0 // MOBA_BLOCK
        q_c = lax.dynamic_slice_in_dim(q, c0, qc_len, axis=2)
        sel_c = lax.dynamic_slice_in_dim(sel, c0, qc_len, axis=2)
        k_sel = kb[b_idx, h_idx, sel_c]
        v_sel = vb[b_idx, h_idx, sel_c]
        sel_pos = sel_c[..., None] * MOBA_BLOCK + blk_off
        s_sel = jnp.einsum('bhqd,bhqknd->bhqkn', q_c, k_sel).astype(jnp.float32) * scale
        s_sel = s_sel + table[h_idx[..., None], rel_bucket(q_pos[:, None, None] - sel_pos)]
        s_sel = jnp.where((sel_c < blk)[..., None], s_sel, NEG_INF)
        own_start = blk * MOBA_BLOCK
        k_own = lax.dynamic_slice_in_dim(k, own_start, MOBA_BLOCK, axis=2)
        v_own = lax.dynamic_slice_in_dim(v, own_start, MOBA_BLOCK, axis=2)
        d_own = q_pos[:, None] - (own_start + blk_off)[None, :]
        s_own = (jnp.einsum('bhqd,bhnd->bhqn', q_c, k_own).astype(jnp.float32) * scale
                 + table[:, rel_bucket(d_own)])
        s_own = jnp.where(d_own >= 0, s_own, NEG_INF)
        logits = jnp.concatenate([s_sel.reshape(B, H, qc_len, n_sel * MOBA_BLOCK), s_own], axis=-1)
        p = jax.nn.softmax(logits, axis=-1).astype(v.dtype)
        p_sel = p[..., :n_sel * MOBA_BLOCK].reshape(B, H, qc_len, n_sel, MOBA_BLOCK)
        p_own = p[..., n_sel * MOBA_BLOCK:]
        return (jnp.einsum('bhqkn,bhqkne->bhqe', p_sel, v_sel)
                + jnp.einsum('bhqn,bhne->bhqe', p_own, v_own))

    out = lax.map(query_chunk, jnp.arange(S // qc_len))
    return out.transpose(1, 0, 3, 2, 4).reshape(B, S, H * dh)


def retention(q, k, v, g, gn_gain):
    B, S, H, dk = q.shape
    dv = v.shape[-1]
    C = RET_CHUNK
    N = S // C
    pos = jnp.arange(S)
    q = rotary(q, pos)
    k = rotary(k, pos) * (dk ** -0.5)
    log_gamma = jnp.log1p(-jnp.power(2.0, -5.0 - jnp.arange(H, dtype=jnp.float32)))
    i = jnp.arange(C, dtype=jnp.float32)
    diff = i[:, None] - i[None, :]
    decay = jnp.where(diff >= 0, jnp.exp(log_gamma[:, None, None] * jnp.maximum(diff, 0.0)), 0.0)
    q_in = jnp.exp(log_gamma[:, None] * (i + 1.0))
    k_out = jnp.exp(log_gamma[:, None] * (C - 1.0 - i))
    chunk_decay = jnp.exp(log_gamma * C)

    def to_chunks(t):
        return t.reshape(B, N, C, H, t.shape[-1]).transpose(0, 3, 1, 2, 4)
    qc, kc, vc = to_chunks(q), to_chunks(k), to_chunks(v)
    scores = jnp.einsum('bhncd,bhnjd->bhncj', qc, kc) * decay[None, :, None]
    inner = jnp.einsum('bhncj,bhnje->bhnce', scores, vc)
    kv = jnp.einsum('bhnjd,bhnje->nbhde', kc * k_out[None, :, None, :, None], vc).astype(jnp.float32)

    def step(state, kv_n):
        return chunk_decay[None, :, None, None] * state + kv_n, state
    _, prev = lax.scan(step, jnp.zeros((B, H, dk, dv), jnp.float32), kv)
    cross = jnp.einsum('bhncd,nbhde->bhnce', qc * q_in[None, :, None, :, None], prev)
    y = (inner + cross).astype(jnp.float32).transpose(0, 2, 3, 1, 4).reshape(B, S, H, dv)
    mu = jnp.mean(y, axis=-1, keepdims=True)
    var = jnp.mean(jnp.square(y - mu), axis=-1, keepdims=True)
    yn = ((y - mu) * lax.rsqrt(var + EPS)).reshape(B, S, H * dv) * gn_gain.astype(jnp.float32)
    return (yn * jax.nn.silu(g.astype(jnp.float32))).astype(v.dtype)


def memory_attention(q, mem_n, w_mem_kv):
    B, S, H, dh = q.shape
    M = mem_n.shape[1]
    k, v = jnp.split(mem_n @ w_mem_kv, 2, axis=-1)
    k = k.reshape(B, M, H, dh)
    v = v.reshape(B, M, H, dh)
    s = jnp.einsum('bshd,bmhd->bhsm', q, k).astype(jnp.float32) * (dh ** -0.5)
    p = jax.nn.softmax(s, axis=-1).astype(v.dtype)
    return jnp.einsum('bhsm,bmhd->bshd', p, v).reshape(B, S, H * dh)


def conv_ffn(x, w_up, conv_w, conv_b, w_down):
    h = x @ w_up
    c = h.shape[-1]
    h = lax.conv_general_dilated(
        h, conv_w.reshape(CONV_WIDTH, 1, c).astype(h.dtype), window_strides=(1,),
        padding=[(CONV_WIDTH - 1, 0)], dimension_numbers=('NWC', 'WIO', 'NWC'),
        feature_group_count=c) + conv_b
    gate, up = jnp.split(h, 2, axis=-1)
    return (jax.nn.gelu(gate, approximate=False) * up) @ w_down


def setup_inputs(seed: int = 0) -> dict:
    key = jax.random.key(seed)
    ks = jax.random.split(key, 18)
    f32 = jnp.float32
    L = DEPTH

    def dense(k, shape, fan_in):
        return jax.random.normal(k, shape, f32) * fan_in ** -0.5

    def gain(k, shape):
        return 1.0 + 0.02 * jax.random.normal(k, shape, f32)

    return {
        "x": jax.random.normal(ks[0], (BATCH, SEQ, D_MODEL), f32),
        "mem": jax.random.normal(ks[1], (BATCH, MEM_LEN, D_MODEL), f32),
        "g_mix": gain(ks[2], (L, D_MODEL)),
        "w_in": dense(ks[3], (L, D_MODEL, D_IN), D_MODEL),
        "rel_bias": 0.5 * jax.random.normal(ks[4], (REL_BUCKETS, MOBA_HEADS), f32),
        "ret_gn_gain": gain(ks[5], (L, RET_V_W)),
        "g_mem": gain(ks[6], (L, D_MODEL)),
        "w_mem_kv": dense(ks[7], (L, D_MODEL, 2 * MEM_W), D_MODEL),
        "w_br_attn": dense(ks[8], (L, MOBA_W, D_MODEL), MOBA_W),
        "w_br_ret": dense(ks[9], (L, RET_V_W, D_MODEL), RET_V_W),
        "w_br_mem": dense(ks[10], (L, MEM_W, D_MODEL), MEM_W),
        "w_out": dense(ks[11], (L, D_MODEL, D_MODEL), D_MODEL),
        "g_ffn": gain(ks[12], (L, D_MODEL)),
        "w_up": dense(ks[13], (L, D_MODEL, 2 * D_FF), D_MODEL),
        "conv_w": dense(ks[14], (L, CONV_WIDTH, 2 * D_FF), CONV_WIDTH),
        "conv_b": 0.02 * jax.random.normal(ks[15], (L, 2 * D_FF), f32),
        "w_down": dense(ks[16], (L, D_FF, D_MODEL), D_FF),
        "g_final": gain(ks[17], (D_MODEL,)),
    }


def reference(x, mem, g_mix, w_in, rel_bias, ret_gn_gain, g_mem, w_mem_kv, w_br_attn, w_br_ret,
              w_br_mem, w_out, g_ffn, w_up, conv_w, conv_b, w_down, g_final):
    B, S, _ = x.shape
    h = x
    for l in range(DEPTH):
        n = rmsnorm(h, g_mix[l])
        (q_a, k_a, v_a, q_r, k_r, v_r, g_r, q_m, z_a, z_r, z_m) = jnp.split(n @ w_in[l], IN_SPLIT_AT, axis=-1)
        y_a = moba_attention(q_a.reshape(B, S, MOBA_HEADS, MOBA_HEAD_DIM),
                             k_a.reshape(B, S, MOBA_HEADS, MOBA_HEAD_DIM),
                             v_a.reshape(B, S, MOBA_HEADS, MOBA_HEAD_DIM), rel_bias)
        y_r = retention(q_r.reshape(B, S, RET_HEADS, RET_QK_DIM),
                        k_r.reshape(B, S, RET_HEADS, RET_QK_DIM),
                        v_r.reshape(B, S, RET_HEADS, RET_V_DIM), g_r, ret_gn_gain[l])
        y_m = memory_attention(q_m.reshape(B, S, MEM_HEADS, MEM_HEAD_DIM),
                               rmsnorm(mem, g_mem[l]), w_mem_kv[l])
        merged = (jax.nn.sigmoid(z_a) * (y_a @ w_br_attn[l])
                  + jax.nn.sigmoid(z_r) * (y_r @ w_br_ret[l])
                  + jax.nn.sigmoid(z_m) * (y_m @ w_br_mem[l]))
        h = h + merged @ w_out[l]
        h = h + conv_ffn(rmsnorm(h, g_ffn[l]), w_up[l], conv_w[l], conv_b[l], w_down[l])
    return rmsnorm(h, g_final)
```

```python
import contextlib
import math
import numpy as np
import concourse.bass as bass
import concourse.mybir as mybir
from concourse.ap import AP
from concourse.bass_utils import run_bass_kernel_spmd

F32 = mybir.dt.float32
BF16 = mybir.dt.bfloat16
AF = mybir.ActivationFunctionType
ALU = mybir.AluOpType
AX = mybir.AxisListType

NB = 8
S = 4096
D = 1024
T = 256
NT = S // T
DIN = 7168
DFF = 2816
MEM = 256
EPS = 1e-6
NEG = -30000.0
EPOCH = 2000
EMBED_WAIT = True
import os as _os
NOSEEN = bool(int(_os.environ.get("KNOSEEN", "0")))


def _rel_bucket_np(d):
    d = np.maximum(d, 0)
    df = np.maximum(d, 1).astype(np.float32)
    large = 16 + (np.log(df / np.float32(16)) / np.float32(math.log(2048 / 16)) * np.float32(16)).astype(np.int32)
    large = np.minimum(large, 31)
    return np.where(d < 16, d, large)


_bk = _rel_bucket_np(np.arange(0, 4096))
D31 = int(np.min(np.nonzero(_bk == 31)[0]))
assert np.all(_bk[D31:] == 31)
DELTA_FAR = ((D31 + 127 + 127) // 128) * 128
TABL = (DELTA_FAR - 128) + 255 + 128 + 1


def _host_consts():
    c = {}
    c["ident"] = np.eye(128, dtype=np.float32)
    jj = np.arange(128)
    c["mask01T"] = (jj[:, None] <= jj[None, :]).astype(np.float32)
    sel = np.zeros((128, 16, 128), np.float32)
    for n in range(16):
        sel[n, n, :] = 1.0
    c["sel"] = sel.reshape(128, 16 * 128)
    cm = np.zeros((128, NT, 16), np.float32)
    for tt in range(NT):
        cm[:, tt, tt:] = -1e30
    c["candmask"] = cm.reshape(128, NT * 16)
    pos = np.arange(S, dtype=np.float64)
    half = 64
    inv = (10000.0 ** (-np.arange(half, dtype=np.float32) / np.float32(half))).astype(np.float32)
    ang = (pos.astype(np.float32)[:, None] * inv[None, :]).astype(np.float32).astype(np.float64)
    cos, sin = np.cos(ang), np.sin(ang)
    p = (np.arange(S) % 128).astype(np.float64)
    lg = np.log1p(-np.power(2.0, -5.0 - np.arange(4, dtype=np.float64)))
    up = np.exp(-lg[None, :] * (127.0 - p[:, None]))
    dn = np.exp(lg[None, :] * (127.0 - p[:, None])) / math.sqrt(128.0)
    c["rt_cos"] = cos.astype(np.float32)
    c["rt_sin"] = sin.astype(np.float32)
    c["scl"] = np.concatenate([up[:128], dn[:128]], axis=1).astype(np.float32)
    c["gamC"] = np.exp(lg * 128.0)
    return c


_HC = _host_consts()
GAMC = [float(v) for v in _HC["gamC"]]


import types as _types


def _freeze(fn):
    if fn is None or fn.__closure__ is None:
        return fn
    cells = []
    for c in fn.__closure__:
        try:
            cells.append(_types.CellType(c.cell_contents))
        except ValueError:
            cells.append(c)
    return _types.FunctionType(fn.__code__, fn.__globals__, fn.__name__, fn.__defaults__, tuple(cells))


class Tk:
    __slots__ = ("name", "w", "r", "alias")

    def __init__(self, name):
        self.name = name
        self.w = None
        self.r = {}
        self.alias = []


class Prog:
    ENG = ("pe", "act", "dve", "pool", "sp")

    def __init__(self):
        self.q = {e: [] for e in self.ENG}
        self.cnt = {e: 0 for e in self.ENG}
        self.seen = {e: {} for e in self.ENG}
        self.dcnt = {}

    def emit(self, eng, fn, reads=(), writes=(), dma=None):
        fn = _freeze(fn)
        deps = {}

        def add(tok):
            if tok is None:
                return
            k, v = tok
            if deps.get(k, 0) < v:
                deps[k] = v

        for t in reads:
            add(t.w)
            for a in t.alias:
                add(a.w)
            if t.name.startswith("ps"):
                for k, v in t.r.items():
                    if k != ("eng", eng):
                        add((k, v))
        for t in writes:
            add(t.w)
            for k, v in t.r.items():
                add((k, v))
            for a in t.alias:
                add(a.w)
                for k, v in a.r.items():
                    add((k, v))
        if dma is not None:
            add((("dma", dma), self.dcnt.get(dma, 0)))
        waits = []
        seen = self.seen[eng]
        for k, v in deps.items():
            if v <= 0:
                continue
            if k == ("eng", "pe") and eng == "pe":
                continue
            if seen.get(k, 0) >= v and not NOSEEN:
                continue
            seen[k] = v
            waits.append((k, v))
        if dma is None:
            self.cnt[eng] += 1
            tok = (("eng", eng), self.cnt[eng])
        else:
            self.dcnt[dma] = self.dcnt.get(dma, 0) + 16
            tok = (("dma", dma), self.dcnt[dma])
        self.q[eng].append((waits, fn, tok))
        for t in reads:
            if t.r.get(tok[0], 0) < tok[1]:
                t.r[tok[0]] = tok[1]
        for t in writes:
            t.w = tok
            t.r = {}
        return tok


def _b(ap, dims):
    return AP(ap.tensor, ap.offset, [list(ap.ap[0])] + [list(d) for d in dims])


def build_program(debug=None, ntiles=NT):
    nc = bass.Bass("TRN2", target_bir_lowering=False)
    P = Prog()

    def din(name, shape, dt=F32):
        return nc.dram_tensor(name, list(shape), dt, kind="ExternalInput").ap()

    x_d = din("x", [S, D])
    mem_d = din("mem", [MEM, D])
    w_in_d = din("w_in", [D, DIN])
    w_kv_d = din("w_mem_kv", [D, 1024])
    w_bra_d = din("w_br_attn", [512, D])
    w_brr_d = din("w_br_ret", [512, D])
    w_brm_d = din("w_br_mem", [512, D])
    w_out_d = din("w_out", [D, D])
    w_up_d = din("w_up", [D, 2 * DFF])
    w_dn_d = din("w_down", [DFF, D])
    gmix_d = din("gmix_col", [128, 8])
    gffn_d = din("gffn_col", [128, 8])
    gmem_d = din("gmem_col", [128, 8])
    gfin_d = din("g_final", [1, D])
    gn_d = din("ret_gn_gain", [1, 512])
    cw_d = din("cw_col", [128, 44 * 3])
    cb_d = din("cb_col", [128, 44])
    rb31_d = din("rb31", [1, 8])
    bsk_d = din("bias_skew", [8, 128, TABL])
    ident_d = din("c_ident", [128, 128])
    m01_d = din("c_mask01T", [128, 128])
    sel_d = din("c_sel", [128, 16 * 128])
    cand_d = din("c_candmask", [128, NT * 16])
    rt_d = [din("c_rt_" + k, [S, 64]) for k in ("cos", "sin")]
    scl_d = din("c_scl", [128, 8])
    y_d = nc.dram_tensor("y", [S, D], F32, kind="ExternalOutput").ap()
    dbg_d = nc.dram_tensor("dbg", [S, D], F32, kind="ExternalOutput").ap() if debug else None

    def dscr(name, shape):
        return nc.dram_tensor(name, list(shape), BF16, kind="Internal").ap()

    wb_in = dscr("wb_in", [D, DIN])
    wb_kv = dscr("wb_kv", [D, 1024])
    wb_bra = dscr("wb_bra", [512, D])
    wb_brr = dscr("wb_brr", [512, D])
    wb_brm = dscr("wb_brm", [512, D])
    wb_out = dscr("wb_out", [D, D])
    wb_up = dscr("wb_up", [D, 2 * DFF])
    wb_dn = dscr("wb_dn", [DFF, D])
    bsk_b = dscr("bsk_b", [8, 128, TABL])

    es = contextlib.ExitStack()
    with es:
        def sb(name, shape, dt):
            return es.enter_context(nc.sbuf_tensor(name, list(shape), dt))

        def sem(name):
            return es.enter_context(nc.semaphore(name))

        kpair = [sb(f"kpair{j}", [128, S], BF16) for j in range(4)]
        Va = sb("Va", [128, 32, 8, 65], BF16)
        xsb = [sb(f"xs{i}", [128, 2, D], F32) for i in range(2)]
        nT = sb("nT", [128, 8, T], BF16)
        wbuf = [sb(f"wbuf{i}", [128, 4096], BF16) for i in range(3)]
        qaug = [sb(f"qaug{h}", [128, T], BF16) for h in range(8)]
        maskT = [sb(f"maskT{h}", [128, T], BF16) for h in range(8)]
        maskpad = sb("maskpad", [128, 8, 128], BF16)
        sel = sb("sel", [128, 16, 128], BF16)
        kmeanT = [sb(f"kmeanT{j}", [128, 16], BF16) for j in range(4)]
        qhT = sb("qhT", [128, 4, T], BF16)
        khT = sb("khT", [128, 4, T], BF16)
        khtok = sb("khtok", [128, 2, 512], BF16)
        vrtok = sb("vrtok", [128, 2, 512], BF16)
        gs = sb("gs", [128, 2, 512], BF16)
        qmT = sb("qmT", [128, 4, T], BF16)
        tz = sb("tz", [128, 3, 8, T], BF16)
        yT = [sb(f"yT{i}", [128, 4, T], BF16) for i in range(3)]
        ytok1 = sb("ytok", [128, 2, 512], BF16)
        ytok = [ytok1, ytok1, ytok1]
        mergedT = sb("mergedT", [128, 8, T], BF16)
        btab = [sb(f"btab{i}", [128, TABL], BF16) for i in range(2)]
        rtab = [sb(f"rtab{i}", [128, 2, 64], F32) for i in range(2)]
        scl = sb("scl", [128, 8], F32)
        sbias = [sb(f"sbias{i}", [128, T], F32) for i in range(2)]
        pT = [sb(f"pT{i}", [128, T], BF16) for i in range(4)]
        KmT = sb("KmT", [128, 4, MEM], BF16)
        Vm = sb("Vm", [128, 2, 4, 129], BF16)
        aT = sb("aT", [128, 22, T], BF16)
        hbuf = [sb(f"hbuf{i}", [128, T + 2], F32) for i in range(2)]
        cacc = [sb(f"cacc{i}", [128, T], F32) for i in range(2)]
        carry = sb("carry", [128, 44, 2], F32)
        Sst = sb("Sst", [128, 4, 128], F32)
        Sbf = sb("Sbf", [128, 4, 128], BF16)
        AT = sb("AT", [128, 4, 128], BF16)
        f32a = sb("f32a", [128, 512], F32)
        f32b = sb("f32b", [128, 512], F32)
        ysq = f32a
        xrot = f32a
        yc = f32b
        tg = f32a
        ug = f32b
        xn = sb("xn", [128, 2, D], BF16)
        stat = sb("stat", [128, 32], F32)
        gsb = sb("gsb", [128, 8, 16], F32)
        top8 = sb("top8", [128, 8, 8], F32)
        rden = sb("rden", [128, 2], F32)
        ident = sb("ident", [128, 128], BF16)
        m01 = sb("m01", [128, 128], F32)
        cand = sb("cand", [128, 16], F32)
        gmix = sb("gmixc", [128, 8], F32)
        gffn = sb("gffnc", [128, 8], F32)
        gmem = sb("gmemc", [128, 8], F32)
        gnh = sb("gnh", [128, 512], F32)
        cw = sb("cw", [128, 44, 3], F32)
        cb = sb("cb", [128, 44], F32)
        rb31 = sb("rb31_sb", [128, 8], F32)
        memnT = aT

        rtmp = [aT[:, 0:2, :].rearrange("p a b -> p (a b)").bitcast(F32), aT[:, 2:4, :].rearrange("p a b -> p (a b)").bitcast(F32)]
        rottok = aT[:, 4:8, :].rearrange("p (s a) b -> p s (a b)", s=2)
        mtmp = f32a[:, 0:T]
        mtmp2 = f32b[:, 0:T]
        gact = f32a[:, T:2 * T]

        psF = [es.enter_context(nc.psum_tensor(f"psF{i}", [128, 512], F32)) for i in range(6)]
        psB = [es.enter_context(nc.psum_tensor(f"psB{i}", [128, 1024], BF16)) for i in range(2)]

        nep = {"pe": 18, "act": 8, "dve": 10, "pool": 6, "sp": 1}
        esem = {e: [sem(f"s_{e}{i}") for i in range(nep[e])] for e in Prog.ENG}
        dsem = {}

        def dma_sem(name):
            if name not in dsem:
                dsem[name] = sem("d_" + name)
            return name

        tk = {}

        def K(obj_name):
            if obj_name not in tk:
                tk[obj_name] = Tk(obj_name)
            return tk[obj_name]

        for _n in ("rtmp0", "rtmp1", "rottok"):
            K(_n).alias.append(K("aT"))
            K("aT").alias.append(K(_n))

        mute = [False]
        kcut = int(_os.environ.get("KCUT", "99"))

        def PE(fn, r=(), w=()):
            if mute[0]:
                return
            P.emit("pe", fn, [K(a) for a in r], [K(a) for a in w])

        def ACT(fn, r=(), w=()):
            if mute[0]:
                return
            P.emit("act", fn, [K(a) for a in r], [K(a) for a in w])

        def DVE(fn, r=(), w=()):
            if mute[0]:
                return
            P.emit("dve", fn, [K(a) for a in r], [K(a) for a in w])

        def POOL(fn, r=(), w=()):
            if mute[0]:
                return
            P.emit("pool", fn, [K(a) for a in r], [K(a) for a in w])

        def DMA(eng, semname, out, in_, r=(), w=()):
            if mute[0]:
                return
            dma_sem(semname)
            P.emit(eng, lambda e, out=out, in_=in_: e.dma_start(out=out, in_=in_),
                   [K(a) for a in r], [K(a) for a in w], dma=semname)

        mmrot = [0]

        def mmbank():
            i = mmrot[0] % 4
            mmrot[0] += 1
            return i

        trrot = [0]

        def trbank():
            i = trrot[0] % 2
            trrot[0] += 1
            return i

        wsrc = {"kv": w_kv_d, "in": w_in_d, "bra": w_bra_d, "brr": w_brr_d, "brm": w_brm_d, "out": w_out_d, "up": w_up_d, "dn": w_dn_d}
        conv_i = [0]
        for h in range(8):
            DMA("pool", f"cv{conv_i[0] % 4}", bsk_b[h], bsk_d[h], r=(), w=(f"cv_bsk{h}",))
            conv_i[0] += 1

        sdesc = []
        for si in range(2):
            sdesc.append(("kv", 0, 8, si * 512, 512))
        NSETUP = len(sdesc)
        for si in range(14):
            sdesc.append(("in", 0, 8, si * 512, 512))
        sdesc += [("bra", 0, 4, 0, 1024), ("brr", 0, 4, 0, 1024), ("brm", 0, 4, 0, 1024)]
        for ch in range(2):
            sdesc.append(("out", 0, 8, ch * 512, 512))
        for pg in range(6):
            ncol = 512 if pg < 5 else 256
            sdesc.append(("up", 0, 8, pg * 512, ncol))
            sdesc.append(("up", 0, 8, DFF + pg * 512, ncol))
        for ch in range(2):
            for kg in range(3):
                sdesc.append(("dn", kg * 8, 8 if kg < 2 else 6, ch * 512, 512))
        NDIST = len(sdesc)
        NPER = NDIST - NSETUP
        wsc = nc.dram_tensor("wsc", [NDIST, 128, 4096], BF16, kind="Internal").ap()
        conv_names = {}

        def ensure_conv(j):
            if j in conv_names:
                return conv_names[j]
            key, r0, nk, c0, ncol = sdesc[j]
            sv = wsrc[key].rearrange("(kc p) c -> p kc c", p=128)
            names = []
            for k0 in range(0, nk, 2):
                kk = min(2, nk - k0)
                name = f"cvs{j}_{k0}"
                DMA("pool", f"cv{conv_i[0] % 6}", wsc[j][:, k0 * ncol:(k0 + kk) * ncol].rearrange("p (k c) -> p k c", k=kk),
                    sv[:, r0 + k0:r0 + k0 + kk, c0:c0 + ncol], r=(), w=(name,))
                conv_i[0] += 1
                names.append(name)
            conv_names[j] = tuple(names)
            return conv_names[j]

        NSLAB = NSETUP + NPER * NT

        def distinct(i):
            return i if i < NSETUP else NSETUP + (i - NSETUP) % NPER

        ws = {"issued": 0, "used": 0}

        def issue_loads(upto):
            for i2 in range(ws["issued"], min(upto + 4, NDIST)):
                ensure_conv(i2)
            while ws["issued"] < min(upto, NSLAB):
                i = ws["issued"]
                j = distinct(i)
                key, r0, nk, c0, ncol = sdesc[j]
                slot = i % 3
                DMA("sp", f"wb{slot}", wbuf[slot][:, 0:nk * ncol], wsc[j][:, 0:nk * ncol], r=ensure_conv(j), w=(f"wbuf{slot}",))
                ws["issued"] += 1

        def next_slab(hold=0):
            i = ws["used"]
            issue_loads(i + 3 - hold)
            ws["used"] += 1
            key, r0, nk, c0, ncol = sdesc[distinct(i)]
            slot = i % 3
            return f"wbuf{slot}", wbuf[slot][:, 0:nk * ncol].rearrange("p (k c) -> p k c", k=nk)

        def ld(dst_ap, src_ap, name, eng="sp"):
            DMA(eng, "c_" + name, dst_ap, src_ap, w=(name,))

        ld(ident[:], ident_d[:, :], "ident", eng="pool")
        ld(m01[:], m01_d[:, :], "m01")
        ld(sel[:].rearrange("p a b -> p (a b)"), sel_d[:, :], "sel", eng="pool")
        ld(scl[:], scl_d[:, :], "scl")
        ld(gmix[:], gmix_d[:, :], "gmix")
        ld(gffn[:], gffn_d[:, :], "gffn")
        ld(gmem[:], gmem_d[:, :], "gmem")
        ld(cw[:].rearrange("p a b -> p (a b)"), cw_d[:, :], "cw")
        ld(cb[:], cb_d[:, :], "cb")
        ld(gnh[:], AP(gn_d.tensor, 0, [[0, 128], [1, 512]]), "gnh")
        ld(rb31[:], AP(rb31_d.tensor, 0, [[0, 128], [1, 8]]), "rb31")
        ld(xsb[0][:], mem_d.rearrange("(s p) d -> p s d", p=128), "xs0")

        POOL(lambda e: e.tensor_scalar(out=gnh[:], in0=gnh[:], scalar1=0.5, scalar2=None, op0=ALU.mult), r=("gnh",), w=("gnh",))
        POOL(lambda e: e.memset(carry[:].rearrange("p a b -> p (a b)"), 0.0), w=("carry",))
        POOL(lambda e: e.memset(Sst[:].rearrange("p a b -> p (a b)"), 0.0), w=("Sst",))
        POOL(lambda e: e.memset(Sbf[:].rearrange("p a b -> p (a b)"), 0.0), w=("Sbf",))
        POOL(lambda e: e.memset(Va[:].rearrange("p a b c -> p (a b c)"), 1.0), w=("Va",))
        POOL(lambda e: e.memset(Vm[:].rearrange("p a b c -> p (a b c)"), 1.0), w=("Vm",))
        for h in range(8):
            POOL(lambda e, h=h: e.memset(qaug[h][:], 0.0), w=(f"qaug{h}",))
        for j in range(4):
            POOL(lambda e, j=j: e.memset(kmeanT[j][:], 0.0), w=(f"kmeanT{j}",))
        POOL(lambda e: e.memset(maskpad[:].rearrange("p a b -> p (a b)"), 0.0), w=("maskpad",))

        def rms_stats(src_name, src, nsub, width):
            for s_ in range(nsub):
                ACT(lambda e, s_=s_: e.activation(out=xn[:, s_, 0:width], in_=src[:, s_, :], func=AF.Square,
                                                  accum_out=stat[:, s_:s_ + 1]),
                    r=(src_name,), w=(f"xn{s_}", f"stat{s_}"))
            names = tuple(f"stat{s_}" for s_ in range(nsub))
            DVE(lambda e: e.tensor_scalar(out=stat[:, 4:4 + nsub], in0=stat[:, 0:nsub], scalar1=1.0 / width,
                                          scalar2=EPS, op0=ALU.mult, op1=ALU.add), r=names, w=("stat_v",))
            ACT(lambda e: e.activation(out=stat[:, 4:4 + nsub], in_=stat[:, 4:4 + nsub], func=AF.Sqrt),
                r=("stat_v",), w=("stat_v",))
            DVE(lambda e: e.reciprocal(out=stat[:, 8:8 + nsub], in_=stat[:, 4:4 + nsub]), r=("stat_v",), w=("stat_r",))

        def norm_p1(src_name, src, ntok):
            nsub = ntok // 128
            rms_stats(src_name, src, nsub, D)
            for s_ in range(nsub):
                DVE(lambda e, s_=s_: e.tensor_scalar(out=xn[:, s_, :], in0=src[:, s_, :], scalar1=stat[:, 8 + s_:9 + s_],
                                                     scalar2=None, op0=ALU.mult), r=(src_name, "stat_r"), w=(f"xn{s_}",))

        def norm_to_T(src_name, src, gcol, gname, dst, dst_name, ntok):
            norm_p1(src_name, src, ntok)
            norm_p2(gcol, gname, dst, dst_name, ntok)

        def norm_p2(gcol, gname, dst, dst_name, ntok):
            nsub = ntok // 128
            for kc in range(8):
                b = trbank()
                for s_ in range(nsub):
                    PE(lambda e, kc=kc, s_=s_, b=b: e.transpose(out=psB[b][:, s_ * 128:(s_ + 1) * 128],
                                                                in_=xn[:, s_, kc * 128:(kc + 1) * 128], identity=ident[:]),
                       r=(f"xn{s_}", "ident"), w=(f"psB{b}",))
                ACT(lambda e, kc=kc, b=b: e.activation(out=dst[:, kc, 0:ntok], in_=psB[b][:, 0:ntok], func=AF.Identity,
                                                        scale=gcol[:, kc:kc + 1]), r=(f"psB{b}", gname), w=(dst_name,))

        def mm_fm(wname, wv, c0, rhs_t, rhs_name, nk, bank, ntok=T):
            for kc in range(nk):
                PE(lambda e, kc=kc: e.matmul(psF[bank][:, 0:ntok], lhsT=wv[:, kc, c0:c0 + 128], rhs=rhs_t[:, kc, 0:ntok],
                                             start=(kc == 0), stop=(kc == nk - 1)),
                   r=(wname, rhs_name), w=(f"psF{bank}",))

        def mm_tm(wname, wv, lhs_t, lhs_name, s_, nk, bank, ncol=512, kofs=0, first=True, last=True):
            for kc in range(nk):
                PE(lambda e, kc=kc: e.matmul(psF[bank][:, 0:ncol], lhsT=lhs_t[:, kofs + kc, s_ * 128:(s_ + 1) * 128],
                                             rhs=wv[:, kc, 0:ncol], start=(first and kc == 0), stop=(last and kc == nk - 1)),
                   r=(wname, lhs_name), w=(f"psF{bank}",))

        def transpose_tok(src, src_name, dst, dst_name, nchunk):
            for j in range(nchunk):
                b = trbank()
                for s_ in range(2):
                    PE(lambda e, j=j, s_=s_, b=b: e.transpose(out=psB[b][:, s_ * 128:(s_ + 1) * 128],
                                                              in_=src[:, s_, j * 128:(j + 1) * 128], identity=ident[:]),
                       r=(src_name, "ident"), w=(f"psB{b}",))
                DVE(lambda e, j=j, b=b: e.tensor_copy(out=dst[:, j, :], in_=psB[b][:, 0:T]), r=(f"psB{b}",), w=(dst_name,))

        norm_to_T("xs0", xsb[0], gmem, "gmem", memnT, "aT", MEM)
        wn, wv = next_slab()
        for h in range(4):
            bk = mmbank()
            mm_fm(wn, wv, h * 128, memnT, "aT", 8, bk, ntok=MEM)
            ACT(lambda e, h=h, bk=bk: e.activation(out=KmT[:, h, :], in_=psF[bk][:, 0:MEM], func=AF.Identity),
                r=(f"psF{bk}",), w=("KmT",))
        wn, wv = next_slab()
        for mc in range(2):
            bk = mmbank()
            mm_tm(wn, wv, memnT, "aT", mc, 8, bk)
            ACT(lambda e, mc=mc, bk=bk: e.activation(out=Vm[:, mc, :, 0:128],
                                                      in_=psF[bk][:, 0:512].rearrange("p (h e) -> p h e", h=4),
                                                      func=AF.Identity), r=(f"psF{bk}",), w=("Vm",))

        LA = 3

        def dbg_store(tag, src_ap, ncol, eng, t0):
            if debug == tag:
                DMA(eng, "dbg", dbg_d[t0:t0 + T, 0:ncol].rearrange("(s p) d -> p s d", p=128), src_ap, r=("ytok", "xs0", "xs1"), w=("dbgdram",))

        _tl = [int(v) for v in _os.environ["KTILES"].split(",")] if _os.environ.get("KTILES") else list(range(ntiles))
        def barrier():
            allt = list(tk.values())
            P.emit("act", lambda e: e.activation(out=stat[:, 30:31], in_=stat[:, 30:31], func=AF.Identity), [], allt)
            P.emit("dve", lambda e: e.memset(stat[:, 31:32], 0.0), [], allt)
            P.emit("pool", lambda e: e.memset(stat[:, 29:30], 0.0), [], allt)
            P.emit("sp", None, [], allt)

        for _ti, tt in enumerate(_tl):
            t0 = tt * T
            if _ti > 0 and _os.environ.get("KBAR"):
                barrier()

            def mark(k, _ti=_ti):
                if _ti > 0 and k >= kcut:
                    mute[0] = True
            X = xsb[tt % 2]
            Xn = f"xs{tt % 2}"
            nxt = _tl[_ti + 1] if _ti + 1 < len(_tl) else None
            if _ti == 0:
                DMA("sp", f"xld{tt % 2}", X[:], x_d[t0:t0 + T, :].rearrange("(s p) d -> p s d", p=128), w=(Xn,))
                norm_to_T(Xn, X, gmix, "gmix", nT, "nT", T)
            if nxt is not None:
                Xq, Xqn = xsb[nxt % 2], f"xs{nxt % 2}"
            if tt > 3:
                DMA("sp", "cand", cand[:], cand_d[:, tt * 16:(tt + 1) * 16], w=("cand",))
            for i in range(2):
                DMA("sp", f"rt{i}", rtab[i][:], rt_d[i][t0:t0 + T, :].rearrange("(s p) c -> p s c", p=128), w=(f"rtab{i}",))
            mark(0)

            mark(1)
            wn, wv = next_slab()
            for j in range(4):
                bk = mmbank()
                mm_fm(wn, wv, j * 128, nT, "nT", 8, bk)
                ACT(lambda e, j=j, bk=bk: e.activation(out=qaug[2 * j][0:64, :], in_=psF[bk][0:64, 0:T], func=AF.Identity,
                                                        scale=0.125), r=(f"psF{bk}",), w=(f"qaug{2 * j}",))
                DVE(lambda e, j=j, bk=bk: e.tensor_scalar(out=qaug[2 * j + 1][64:128, :], in0=psF[bk][64:128, 0:T], scalar1=0.125,
                                                           scalar2=None, op0=ALU.mult), r=(f"psF{bk}",), w=(f"qaug{2 * j + 1}",))
            wn, wv = next_slab()
            for j in range(4):
                bk = mmbank()
                mm_fm(wn, wv, j * 128, nT, "nT", 8, bk)
                ACT(lambda e, j=j, bk=bk: e.activation(out=kpair[j][:, t0:t0 + T], in_=psF[bk][:, 0:T], func=AF.Identity),
                    r=(f"psF{bk}",), w=(f"kpair{j}",))
                DVE(lambda e, bk=bk: e.tensor_reduce(out=stat[:, 12:13], in_=psF[bk][:, 0:T], axis=AX.X, op=ALU.add),
                    r=(f"psF{bk}",), w=("stat_k",))
                DVE(lambda e, j=j: e.tensor_scalar(out=kmeanT[j][:, tt:tt + 1], in0=stat[:, 12:13], scalar1=1.0 / T, scalar2=None,
                                                    op0=ALU.mult), r=("stat_k",), w=(f"kmeanT{j}",))
            wn, wv = next_slab()
            for s_ in range(2):
                bk = mmbank()
                mm_tm(wn, wv, nT, "nT", s_, 8, bk)
                ACT(lambda e, s_=s_, bk=bk: e.activation(out=Va[:, 2 * tt + s_, :, 0:64],
                                                          in_=psF[bk][:, 0:512].rearrange("p (h e) -> p h e", h=8),
                                                          func=AF.Identity), r=(f"psF{bk}",), w=("Va",))
            def gate_p1(s_):
                bk = mmbank()
                for h in range(8):
                    PE(lambda e, h=h: e.matmul(psF[bk][:, h * 16:(h + 1) * 16], lhsT=qaug[h][:, s_ * 128:(s_ + 1) * 128],
                                               rhs=kmeanT[h // 2][:, 0:16], start=True, stop=True),
                       r=(f"qaug{h}", f"kmeanT{h // 2}"), w=(f"psF{bk}",))
                cmv = _b(cand[:, :], [[0, 8], [1, 16]])
                DVE(lambda e: e.tensor_tensor(out=gsb[:], in0=psF[bk][:, 0:128].rearrange("p (h n) -> p h n", h=8),
                                              in1=cmv, op=ALU.add), r=(f"psF{bk}", "cand"), w=("gsb",))
                for h in range(8):
                    DVE(lambda e, h=h: e.max(out=top8[:, h, :], in_=gsb[:, h, :]), r=("gsb",), w=("top8",))
                for h in range(8):
                    DVE(lambda e, h=h: e.tensor_scalar(out=maskpad[:, h, 0:16], in0=gsb[:, h, :], scalar1=top8[:, h, 2:3],
                                                       scalar2=None, op0=ALU.is_ge), r=("gsb", "top8"), w=("maskpad",))
                DVE(lambda e: e.tensor_scalar(out=maskpad[:, :, 0:16], in0=maskpad[:, :, 0:16], scalar1=-1.0,
                                              scalar2=-NEG, op0=ALU.add, op1=ALU.mult), r=("maskpad",), w=("maskpad",))

            def gate_p2(s_):
                for h in range(8):
                    b = trbank()
                    PE(lambda e, h=h: e.transpose(out=psB[b][:, 0:128], in_=maskpad[:, h, :], identity=ident[:]),
                       r=("maskpad", "ident"), w=(f"psB{b}",))
                    ACT(lambda e, h=h: e.activation(out=maskT[h][:, s_ * 128:(s_ + 1) * 128], in_=psB[b][:, 0:128], func=AF.Identity),
                        r=(f"psB{b}",), w=(f"maskT{h}",))

            def ret_a(s_):
                bk = mmbank()
                for h in range(4):
                    PE(lambda e, h=h, s_=s_, bk=bk: e.matmul(psF[bk][:, h * 128:(h + 1) * 128], lhsT=khT[:, h, s_ * 128:(s_ + 1) * 128],
                                                              rhs=qhT[:, h, s_ * 128:(s_ + 1) * 128], start=True, stop=True),
                       r=("khT", "qhT"), w=(f"psF{bk}",))
                DVE(lambda e, bk=bk: e.tensor_tensor(out=AT[:], in0=psF[bk][:, 0:512].rearrange("p (h c) -> p h c", h=4),
                                                     in1=_b(m01[:], [[0, 4], [1, 128]]), op=ALU.mult), r=(f"psF{bk}", "m01"), w=("AT",))
                return bk

            def ret_b(s_):
                for h in range(4):
                    PE(lambda e, h=h, s_=s_: e.matmul(psF[4][:, h * 128:(h + 1) * 128], lhsT=AT[:, h, :], rhs=vrtok[:, s_, h * 128:(h + 1) * 128],
                                                       start=True, stop=False), r=("AT", "vrtok"), w=("psF4",))
                    PE(lambda e, h=h, s_=s_: e.matmul(psF[4][:, h * 128:(h + 1) * 128], lhsT=qhT[:, h, s_ * 128:(s_ + 1) * 128], rhs=Sbf[:, h, :],
                                                       start=False, stop=True), r=("qhT", "Sbf"), w=("psF4",))
                for h in range(4):
                    PE(lambda e, h=h, s_=s_: e.matmul(psF[5][:, h * 128:(h + 1) * 128], lhsT=khtok[:, s_, h * 128:(h + 1) * 128],
                                                       rhs=vrtok[:, s_, h * 128:(h + 1) * 128], start=True, stop=True),
                       r=("khtok", "vrtok"), w=("psF5",))
                for h in range(4):
                    DVE(lambda e, h=h: e.scalar_tensor_tensor(out=Sst[:, h, :], in0=Sst[:, h, :], scalar=GAMC[h], in1=psF[5][:, h * 128:(h + 1) * 128],
                                                              op0=ALU.mult, op1=ALU.add), r=("Sst", "psF5"), w=("Sst",))
                for h in range(4):
                    POOL(lambda e, h=h: e.tensor_scalar(out=Sbf[:, h, :], in0=Sst[:, h, :], scalar1=GAMC[h], scalar2=None, op0=ALU.mult),
                         r=("Sst",), w=("Sbf",))
                pyv = psF[4][:, 0:512].rearrange("p (h e) -> p h e", h=4)
                DVE(lambda e, pyv=pyv: e.tensor_reduce(out=stat[:, 16:20], in_=pyv, axis=AX.X, op=ALU.add), r=("psF4",), w=("gn_s",))
                ACT(lambda e: e.activation(out=ysq[:], in_=psF[4][:, 0:512], func=AF.Square), r=("psF4",), w=("f32a",))
                DVE(lambda e: e.tensor_reduce(out=stat[:, 20:24], in_=ysq[:].rearrange("p (h e) -> p h e", h=4), axis=AX.X, op=ALU.add),
                    r=("f32a",), w=("gn_q",))
                DVE(lambda e: e.tensor_scalar(out=stat[:, 16:20], in0=stat[:, 16:20], scalar1=1.0 / 128, scalar2=None, op0=ALU.mult),
                    r=("gn_s",), w=("gn_s",))
                DVE(lambda e: e.tensor_tensor(out=stat[:, 24:28], in0=stat[:, 16:20], in1=stat[:, 16:20], op=ALU.mult), r=("gn_s",), w=("gn_m2",))
                DVE(lambda e: e.scalar_tensor_tensor(out=stat[:, 20:24], in0=stat[:, 20:24], scalar=1.0 / 128, in1=stat[:, 24:28],
                                                     op0=ALU.mult, op1=ALU.subtract), r=("gn_q", "gn_m2"), w=("gn_q",))
                DVE(lambda e: e.tensor_scalar(out=stat[:, 20:24], in0=stat[:, 20:24], scalar1=EPS, scalar2=None, op0=ALU.add),
                    r=("gn_q",), w=("gn_q",))
                ACT(lambda e: e.activation(out=stat[:, 20:24], in_=stat[:, 20:24], func=AF.Sqrt), r=("gn_q",), w=("gn_q",))
                DVE(lambda e: e.reciprocal(out=stat[:, 24:28], in_=stat[:, 20:24]), r=("gn_q", "gn_m2"), w=("gn_m2",))
                ycv = yc[:].rearrange("p (h e) -> p h e", h=4)
                DVE(lambda e, pyv=pyv, ycv=ycv: e.tensor_tensor(out=ycv, in0=pyv, in1=_b(stat[:, 16:20], [[1, 4], [0, 128]]), op=ALU.subtract),
                    r=("psF4", "gn_s"), w=("f32b",))
                POOL(lambda e, ycv=ycv: e.tensor_tensor(out=ycv, in0=ycv, in1=_b(stat[:, 24:28], [[1, 4], [0, 128]]), op=ALU.mult),
                     r=("f32b", "gn_m2"), w=("f32b",))
                POOL(lambda e, s_=s_: e.tensor_tensor(out=ytok[1][:, s_, :], in0=yc[:], in1=gs[:, s_, :], op=ALU.mult),
                     r=("f32b", "gs"), w=("ytok",))


            def mem_attn():
                mitems = [(h, mc) for h in range(4) for mc in range(2)]

                def mem_s1(h, mc, idx):
                    bk = mmbank()
                    PE(lambda e: e.matmul(psF[bk][:, 0:T], lhsT=KmT[:, h, mc * 128:(mc + 1) * 128], rhs=qmT[:, h, :], start=True, stop=True),
                       r=("KmT", "qmT"), w=(f"psF{bk}",))
                    pi = idx % 4
                    ACT(lambda e: e.activation(out=pT[pi][:], in_=psF[bk][:, 0:T], func=AF.Exp, scale=128.0 ** -0.5),
                        r=(f"psF{bk}",), w=(f"pT{pi}",))

                def mem_s2(h, mc, idx):
                    acc = 4 + (h % 2)
                    pi = idx % 4
                    for s_ in range(2):
                        PE(lambda e, s_=s_: e.matmul(psF[acc][:, s_ * 129:(s_ + 1) * 129], lhsT=pT[pi][:, s_ * 128:(s_ + 1) * 128], rhs=Vm[:, mc, h, :],
                                                     start=(mc == 0 and s_ == 0), stop=(mc == 1), skip_group_check=True),
                           r=(f"pT{pi}", "Vm"), w=(f"psF{acc}",))
                    if mc == 1:
                        pov = psF[acc][:, 0:258].rearrange("p (s e) -> p s e", s=2)
                        DVE(lambda e: e.reciprocal(out=rden[:], in_=pov[:, :, 128]), r=(f"psF{acc}",), w=("rden",))
                        DVE(lambda e: e.tensor_tensor(out=ytok[2][:, :, h * 128:(h + 1) * 128], in0=pov[:, :, 0:128],
                                                      in1=_b(rden[:], [[1, 2], [0, 128]]), op=ALU.mult),
                            r=(f"psF{acc}", "rden"), w=("ytok",))

                for i in range(len(mitems) + LA):
                    if i < len(mitems):
                        mem_s1(mitems[i][0], mitems[i][1], i)
                    if i >= LA:
                        mem_s2(mitems[i - LA][0], mitems[i - LA][1], i - LA)


            def rot_transposes(which):
                src_t = rottok if which == 0 else khtok
                src_name = "rottok" if which == 0 else "khtok"
                dstT = qhT if which == 0 else khT
                dstT_name = "qhT" if which == 0 else "khT"
                for h in range(4):
                    b = trbank()
                    for s_ in range(2):
                        PE(lambda e, s_=s_: e.transpose(out=psB[b][:, s_ * 128:(s_ + 1) * 128], in_=src_t[:, s_, h * 128:(h + 1) * 128],
                                                        identity=ident[:]), r=(src_name, "ident"), w=(f"psB{b}",))
                    ACT(lambda e: e.activation(out=dstT[:, h, :], in_=psB[b][:, 0:T], func=AF.Identity), r=(f"psB{b}",), w=(dstT_name,))

            if tt > 3:
                gate_p1(0)
            for which in range(2):
                wn, wv = next_slab()
                for s_ in range(2):
                    bk = mmbank()
                    mm_tm(wn, wv, nT, "nT", s_, 8, bk)
                    DVE(lambda e, bk=bk, which=which: e.tensor_tensor(out=xrot[:].rearrange("p (h d) -> p h d", h=4),
                                                                      in0=psF[bk][:, 0:512].rearrange("p (h d) -> p h d", h=4),
                                                                      in1=_b(scl[:, which * 4:which * 4 + 4], [[1, 4], [0, 128]]), op=ALU.mult),
                        r=(f"psF{bk}", "scl"), w=("f32a",))
                    xv = xrot[:].rearrange("p (h t i) -> p h t i", h=4, t=2)
                    Ct = _b(rtab[0][:, s_, :], [[0, 4], [1, 64]])
                    St = _b(rtab[1][:, s_, :], [[0, 4], [1, 64]])
                    r4 = [rtmp[i][:].rearrange("p (h i) -> p h i", h=4) for i in range(2)]
                    dst_tok = khtok[:, s_, :] if which == 1 else rottok[:, s_, :]
                    dst_name = "khtok" if which == 1 else "rottok"
                    ov = dst_tok.rearrange("p (h t i) -> p h t i", h=4, t=2)
                    DVE(lambda e, xv=xv, Ct=Ct, r4=r4: e.tensor_tensor(out=r4[0], in0=xv[:, :, 0, :], in1=Ct, op=ALU.mult),
                        r=("f32a", "rtab0"), w=("rtmp0",))
                    POOL(lambda e, xv=xv, St=St, r4=r4: e.tensor_tensor(out=r4[1], in0=xv[:, :, 1, :], in1=St, op=ALU.mult),
                         r=("f32a", "rtab1"), w=("rtmp1",))
                    DVE(lambda e, ov=ov, r4=r4: e.tensor_tensor(out=ov[:, :, 0, :], in0=r4[0], in1=r4[1], op=ALU.subtract),
                        r=("rtmp0", "rtmp1"), w=(dst_name,))
                    POOL(lambda e, xv=xv, St=St, r4=r4: e.tensor_tensor(out=r4[0], in0=xv[:, :, 0, :], in1=St, op=ALU.mult),
                         r=("f32a", "rtab1"), w=("rtmp0",))
                    DVE(lambda e, xv=xv, Ct=Ct, r4=r4: e.tensor_tensor(out=r4[1], in0=xv[:, :, 1, :], in1=Ct, op=ALU.mult),
                        r=("f32a", "rtab0"), w=("rtmp1",))
                    POOL(lambda e, ov=ov, r4=r4: e.tensor_tensor(out=ov[:, :, 1, :], in0=r4[0], in1=r4[1], op=ALU.add),
                         r=("rtmp0", "rtmp1"), w=(dst_name,))
                if tt > 3:
                    if which == 0:
                        gate_p2(0)
                        gate_p1(1)
                    else:
                        gate_p2(1)
                if which == 1:
                    rot_transposes(0)
            wn, wv = next_slab()
            for s_ in range(2):
                bk = mmbank()
                mm_tm(wn, wv, nT, "nT", s_, 8, bk)
                ACT(lambda e, s_=s_, bk=bk: e.activation(out=vrtok[:, s_, :], in_=psF[bk][:, 0:512], func=AF.Identity),
                    r=(f"psF{bk}",), w=("vrtok",))
            rot_transposes(1)
            wn, wv = next_slab()
            for s_ in range(2):
                bk = mmbank()
                mm_tm(wn, wv, nT, "nT", s_, 8, bk)
                ACT(lambda e, bk=bk: e.activation(out=tg[:], in_=psF[bk][:, 0:512], func=AF.Tanh, scale=0.5),
                    r=(f"psF{bk}",), w=("f32a",))
                DVE(lambda e, bk=bk: e.scalar_tensor_tensor(out=ug[:], in0=tg[:], scalar=1.0, in1=psF[bk][:, 0:512],
                                                            op0=ALU.add, op1=ALU.mult), r=("f32a", f"psF{bk}"), w=("f32b",))
                POOL(lambda e, s_=s_: e.tensor_tensor(out=gs[:, s_, :], in0=ug[:], in1=gnh[:], op=ALU.mult),
                     r=("f32b", "gnh"), w=("gs",))
            wn, wv = next_slab()
            for h in range(4):
                bk = mmbank()
                mm_fm(wn, wv, h * 128, nT, "nT", 8, bk)
                ACT(lambda e, h=h, bk=bk: e.activation(out=qmT[:, h, :], in_=psF[bk][:, 0:T], func=AF.Identity),
                    r=(f"psF{bk}",), w=("qmT",))
            ret_a(0)
            for zi in range(3):
                for half in range(2):
                    wn, wv = next_slab()
                    for j in range(4):
                        bk = mmbank()
                        mm_fm(wn, wv, j * 128, nT, "nT", 8, bk)
                        ACT(lambda e, zi=zi, fc=half * 4 + j, bk=bk: e.activation(out=tz[:, zi, fc, :], in_=psF[bk][:, 0:T],
                                                                                func=AF.Tanh, scale=0.5),
                            r=(f"psF{bk}",), w=("tz",))
                    zs = zi * 2 + half
                    if zs == 0:
                        ret_b(0)
                    elif zs == 1:
                        ret_a(1)
                    elif zs == 2:
                        ret_b(1)
                    elif zs == 3:
                        transpose_tok(ytok[1], "ytok", yT[1], "yT1", 4)
                        dbg_store("yr", ytok[1][:], 512, "pool", t0)
                    elif zs == 4:
                        mem_attn()
                    else:
                        transpose_tok(ytok[2], "ytok", yT[2], "yT2", 4)
                        dbg_store("ym", ytok[2][:], 512, "pool", t0)

            mark(2)
            nchunk = 2 * tt + 2
            items = [(h, c) for h in range(8) for c in range(nchunk)]

            def moba_s1(h, c, idx):
                bb = h % 2
                if c == 0:
                    if h == 0:
                        DMA("sp", "bt0", btab[0][:], bsk_b[0], r=("cv_bsk0",), w=("btab0",))
                    if h + 1 < 8:
                        DMA("sp", f"bt{(h + 1) % 2}", btab[(h + 1) % 2][:], bsk_b[h + 1], r=(f"cv_bsk{h + 1}",), w=(f"btab{(h + 1) % 2}",))
                k0 = c * 128
                delta = t0 - k0
                use_mask = ((c // 2) < tt) and tt > 3
                bk = mmbank()
                PE(lambda e: e.matmul(psF[bk][:, 0:T], lhsT=kpair[h // 2][:, k0:k0 + 128], rhs=qaug[h][:], start=True, stop=not use_mask),
                   r=(f"kpair{h // 2}", f"qaug{h}"), w=(f"psF{bk}",))
                if use_mask:
                    PE(lambda e: e.matmul(psF[bk][:, 0:T], lhsT=sel[:, c // 2, :], rhs=maskT[h][:], start=False, stop=True),
                       r=("sel", f"maskT{h}"), w=(f"psF{bk}",))
                pi = idx % 4
                if delta >= DELTA_FAR:
                    ACT(lambda e: e.activation(out=pT[pi][:], in_=psF[bk][:, 0:T], func=AF.Exp, bias=rb31[:, h:h + 1]),
                        r=(f"psF{bk}", "rb31"), w=(f"pT{pi}",))
                else:
                    j0 = delta + 128
                    si2 = idx % 2
                    DVE(lambda e: e.tensor_tensor(out=sbias[si2][:], in0=psF[bk][:, 0:T], in1=btab[bb][:, j0:j0 + T], op=ALU.add),
                        r=(f"psF{bk}", f"btab{bb}"), w=(f"sbias{si2}",))
                    ACT(lambda e: e.activation(out=pT[pi][:], in_=sbias[si2][:], func=AF.Exp), r=(f"sbias{si2}",), w=(f"pT{pi}",))

            def moba_s2(h, c, idx):
                acc = 4 + (h % 2)
                pi = idx % 4
                for s_ in range(2):
                    PE(lambda e, s_=s_: e.matmul(psF[acc][:, s_ * 65:(s_ + 1) * 65], lhsT=pT[pi][:, s_ * 128:(s_ + 1) * 128], rhs=Va[:, c, h, :],
                                                 start=(c == 0 and s_ == 0), stop=(c == nchunk - 1), skip_group_check=True),
                       r=(f"pT{pi}", "Va"), w=(f"psF{acc}",))
                if c == nchunk - 1:
                    pov = psF[acc][:, 0:130].rearrange("p (s e) -> p s e", s=2)
                    DVE(lambda e: e.reciprocal(out=rden[:], in_=pov[:, :, 64]), r=(f"psF{acc}",), w=("rden",))
                    DVE(lambda e: e.tensor_tensor(out=ytok[0][:, :, h * 64:(h + 1) * 64], in0=pov[:, :, 0:64],
                                                  in1=_b(rden[:], [[1, 2], [0, 64]]), op=ALU.mult),
                        r=(f"psF{acc}", "rden"), w=("ytok",))

            def ya_pair_T(j):
                b = trbank()
                for s_ in range(2):
                    PE(lambda e, s_=s_: e.transpose(out=psB[b][:, s_ * 128:(s_ + 1) * 128], in_=ytok[0][:, s_, j * 128:(j + 1) * 128],
                                                    identity=ident[:]), r=("ytok", "ident"), w=(f"psB{b}",))
                DVE(lambda e: e.tensor_copy(out=yT[0][:, j, :], in_=psB[b][:, 0:T]), r=(f"psB{b}",), w=("yT0",))

            pend = []
            for i in range(len(items) + LA):
                if i < len(items):
                    moba_s1(items[i][0], items[i][1], i)
                if i >= LA:
                    hh, cc = items[i - LA]
                    moba_s2(hh, cc, i - LA)
                    if cc == nchunk - 1 and hh % 2 == 1:
                        pend.append((i + max(2, nchunk // 2), hh // 2))
                while pend and pend[0][0] <= i:
                    ya_pair_T(pend.pop(0)[1])
            for _, j in pend:
                ya_pair_T(j)
            dbg_store("ya", ytok[0][:], 512, "pool", t0)

            mark(5)
            if nxt is not None:
                DMA("sp", f"xld{nxt % 2}", Xq[:], x_d[nxt * T:(nxt + 1) * T, :].rearrange("(s p) d -> p s d", p=128), w=(Xqn,))
            wsl = [next_slab(hold=bi) for bi in range(3)]
            for fc in range(8):
                bks = []
                for bi in range(3):
                    bk = mmbank()
                    bks.append(bk)
                    mm_fm(wsl[bi][0], wsl[bi][1], fc * 128, yT[bi], f"yT{bi}", 4, bk)
                DVE(lambda e, fc=fc, bk=bks[0]: e.scalar_tensor_tensor(out=mtmp[:], in0=tz[:, 0, fc, :], scalar=1.0, in1=psF[bk][:, 0:T],
                                                                       op0=ALU.add, op1=ALU.mult), r=("tz", f"psF{bks[0]}"), w=("f32a",))
                DVE(lambda e, fc=fc, bk=bks[1]: e.scalar_tensor_tensor(out=mtmp2[:], in0=tz[:, 1, fc, :], scalar=1.0, in1=psF[bk][:, 0:T],
                                                                       op0=ALU.add, op1=ALU.mult), r=("tz", f"psF{bks[1]}"), w=("f32b",))
                POOL(lambda e: e.tensor_tensor(out=mtmp[:], in0=mtmp[:], in1=mtmp2[:], op=ALU.add), r=("f32a", "f32b"), w=("f32a",))
                DVE(lambda e, fc=fc, bk=bks[2]: e.scalar_tensor_tensor(out=mtmp2[:], in0=tz[:, 2, fc, :], scalar=1.0, in1=psF[bk][:, 0:T],
                                                                       op0=ALU.add, op1=ALU.mult), r=("tz", f"psF{bks[2]}"), w=("f32b",))
                POOL(lambda e, fc=fc: e.tensor_tensor(out=mergedT[:, fc, :], in0=mtmp[:], in1=mtmp2[:], op=ALU.add),
                     r=("f32a", "f32b"), w=("mergedT",))
            for ch in range(2):
                wn, wv = next_slab()
                for s_ in range(2):
                    bk = mmbank()
                    mm_tm(wn, wv, mergedT, "mergedT", s_, 8, bk)
                    DVE(lambda e, s_=s_, ch=ch, bk=bk: e.scalar_tensor_tensor(out=X[:, s_, ch * 512:(ch + 1) * 512], in0=psF[bk][:, 0:512], scalar=0.5,
                                                                              in1=X[:, s_, ch * 512:(ch + 1) * 512], op0=ALU.mult, op1=ALU.add),
                        r=(f"psF{bk}", Xn), w=(Xn,))

            dbg_store("h2", X[:], 1024, "sp", t0)
            mark(6)
            norm_to_T(Xn, X, gffn, "gffn", nT, "nT", T)
            for pg in range(6):
                npair = 4 if pg < 5 else 2
                wn_g, wv_g = next_slab()
                wn_u, wv_u = next_slab(hold=1)
                for pj in range(npair):
                    i = pg * 4 + pj
                    for which, (wn_, wv_) in enumerate(((wn_g, wv_g), (wn_u, wv_u))):
                        chn = i + 22 * which
                        bk = mmbank()
                        mm_fm(wn_, wv_, pj * 128, nT, "nT", 8, bk)
                        hb, ca = hbuf[which], cacc[which]
                        POOL(lambda e, hb=hb, chn=chn: e.tensor_copy(out=hb[:, 0:2], in_=carry[:, chn, :]), r=("carry",), w=(f"hbuf{which}",))
                        ACT(lambda e, hb=hb, bk=bk: e.activation(out=hb[:, 2:T + 2], in_=psF[bk][:, 0:T], func=AF.Identity),
                            r=(f"psF{bk}",), w=(f"hbuf{which}",))
                        ACT(lambda e, ca=ca, bk=bk, chn=chn: e.activation(out=ca[:], in_=psF[bk][:, 0:T], func=AF.Identity,
                                                                          scale=cw[:, chn, 2:3], bias=cb[:, chn:chn + 1]),
                            r=(f"psF{bk}", "cw", "cb"), w=(f"cacc{which}",))
                        DVE(lambda e, ca=ca, hb=hb, chn=chn: e.scalar_tensor_tensor(out=ca[:], in0=hb[:, 1:T + 1], scalar=cw[:, chn, 1:2], in1=ca[:],
                                                                                    op0=ALU.mult, op1=ALU.add),
                            r=(f"hbuf{which}", "cw", f"cacc{which}"), w=(f"cacc{which}",))
                        DVE(lambda e, ca=ca, hb=hb, chn=chn: e.scalar_tensor_tensor(out=ca[:], in0=hb[:, 0:T], scalar=cw[:, chn, 0:1], in1=ca[:],
                                                                                     op0=ALU.mult, op1=ALU.add),
                             r=(f"hbuf{which}", "cw", f"cacc{which}"), w=(f"cacc{which}",))
                        POOL(lambda e, hb=hb, chn=chn: e.tensor_copy(out=carry[:, chn, :], in_=hb[:, T:T + 2]), r=(f"hbuf{which}",), w=("carry",))
                    ACT(lambda e: e.activation(out=gact[:], in_=cacc[0][:], func=AF.Gelu), r=("cacc0",), w=("f32a",))
                    DVE(lambda e, i=i: e.tensor_tensor(out=aT[:, i, :], in0=gact[:], in1=cacc[1][:], op=ALU.mult), r=("f32a", "cacc1"), w=("aT",))
            if nxt is not None:
                norm_p1(Xqn, Xq, T)
            for ch in range(2):
                for kg in range(3):
                    nk = 8 if kg < 2 else 6
                    wn, wv = next_slab()
                    for s_ in range(2):
                        mm_tm(wn, wv, aT, "aT", s_, nk, 4 + s_, kofs=kg * 8, first=(kg == 0), last=(kg == 2))
                for s_ in range(2):
                    DVE(lambda e, s_=s_, ch=ch: e.tensor_tensor(out=X[:, s_, ch * 512:(ch + 1) * 512], in0=psF[4 + s_][:, 0:512],
                                                                in1=X[:, s_, ch * 512:(ch + 1) * 512], op=ALU.add),
                        r=(f"psF{4 + s_}", Xn), w=(Xn,))
            if nxt is not None:
                norm_p2(gmix, "gmix", nT, "nT", T)
            dbg_store("h3", X[:], 1024, "sp", t0)
            mute[0] = False
            DMA("sp", "gf0", f32a[:], AP(gfin_d.tensor, 0, [[0, 128], [1, 512]]), w=("f32a",))
            DMA("sp", "gf1", f32b[:], AP(gfin_d.tensor, 512, [[0, 128], [1, 512]]), w=("f32b",))
            rms_stats(Xn, X, 2, D)
            for s_ in range(2):
                DVE(lambda e, s_=s_: e.tensor_scalar(out=X[:, s_, :], in0=X[:, s_, :], scalar1=stat[:, 8 + s_:9 + s_], scalar2=None, op0=ALU.mult),
                    r=(Xn, "stat_r"), w=(Xn,))
                POOL(lambda e, s_=s_: e.tensor_tensor(out=X[:, s_, 0:512], in0=X[:, s_, 0:512], in1=f32a[:], op=ALU.mult), r=(Xn, "f32a"), w=(Xn,))
                POOL(lambda e, s_=s_: e.tensor_tensor(out=X[:, s_, 512:1024], in0=X[:, s_, 512:1024], in1=f32b[:], op=ALU.mult), r=(Xn, "f32b"), w=(Xn,))
            DMA("sp", f"st{tt % 2}", y_d[t0:t0 + T, :].rearrange("(s p) d -> p s d", p=128), X[:], r=(Xn,), w=("ydram",))

        P.emit("sp", None, [K("ydram"), K("xs0"), K("xs1"), K("dbgdram")], [K("xs0"), K("xs1"), K("ytok")])

        waited = {e_: set() for e_ in Prog.ENG}
        for e_ in Prog.ENG:
            for waits_, _fn, _tok in P.q[e_]:
                for k_, v_ in waits_:
                    if k_[0] == "eng":
                        waited[k_[1]].add(v_)
        rank = {e_: {v_: i_ + 1 for i_, v_ in enumerate(sorted(waited[e_]))} for e_ in Prog.ENG}

        def semfor(key, val):
            kind, nm = key
            if kind == "dma":
                return dsem[nm], val
            r_ = rank[nm][val]
            ep = (r_ - 1) // EPOCH
            return esem[nm][ep], r_ - ep * EPOCH

        block = es.enter_context(nc.Block())

        def replay(eng_name, eng):
            for waits, fn, tok in P.q[eng_name]:
                emb = None
                if fn is not None and waits and EMBED_WAIT and eng_name != "pe":
                    emb = waits[-1]
                    waits = waits[:-1]
                for k, v in waits:
                    s_h, s_v = semfor(k, v)
                    eng.wait_ge(s_h, s_v)
                if fn is None:
                    continue
                ins = fn(eng)
                if emb is not None:
                    s_h, s_v = semfor(emb[0], emb[1])
                    ins._wait_ge(s_h, s_v)
                if tok[0][0] == "dma":
                    ins.then_inc(dsem[tok[0][1]], 16)
                elif tok[1] in rank[tok[0][1]]:
                    s_h, _ = semfor(tok[0], tok[1])
                    ins.then_inc(s_h, 1)

        for e_ in Prog.ENG:
            assert (len(rank[e_]) + EPOCH - 1) // EPOCH <= nep[e_], (e_, len(rank[e_]))

        @block.sync
        def _(e):
            replay("sp", e)

        @block.tensor
        def _(e):
            replay("pe", e)

        @block.scalar
        def _(e):
            replay("act", e)

        @block.vector
        def _(e):
            replay("dve", e)

        @block.gpsimd
        def _(e):
            replay("pool", e)

    return nc, P


def _prep_inputs(inputs):
    f = lambda a: np.ascontiguousarray(np.asarray(a, dtype=np.float32))
    col = lambda g: np.ascontiguousarray(f(g).reshape(8, 128).T)
    rel_bias = f(inputs["rel_bias"])
    kk = np.arange(128)[:, None]
    jj = np.arange(TABL)[None, :]
    d = jj - 128 - kk
    bidx = _rel_bucket_np(d)
    bsk = np.empty((8, 128, TABL), np.float32)
    for h in range(8):
        bsk[h] = np.where(d >= 0, rel_bias[bidx, h], np.float32(NEG))
    conv_w = f(inputs["conv_w"])[0]
    conv_b = f(inputs["conv_b"])[0]
    cw = np.ascontiguousarray(conv_w.reshape(3, 44, 128).transpose(2, 1, 0)).reshape(128, 44 * 3)
    cb = np.ascontiguousarray(conv_b.reshape(44, 128).T)
    shared = {
        "w_in": f(inputs["w_in"])[0], "w_mem_kv": f(inputs["w_mem_kv"])[0],
        "w_br_attn": f(inputs["w_br_attn"])[0], "w_br_ret": f(inputs["w_br_ret"])[0], "w_br_mem": f(inputs["w_br_mem"])[0],
        "w_out": f(inputs["w_out"])[0], "w_up": f(inputs["w_up"])[0], "w_down": f(inputs["w_down"])[0],
        "gmix_col": col(inputs["g_mix"]), "gffn_col": col(inputs["g_ffn"]), "gmem_col": col(inputs["g_mem"]),
        "g_final": f(inputs["g_final"]).reshape(1, D), "ret_gn_gain": f(inputs["ret_gn_gain"]).reshape(1, 512),
        "cw_col": cw, "cb_col": cb, "rb31": np.ascontiguousarray(rel_bias[31:32, :]), "bias_skew": bsk,
        "c_ident": _HC["ident"], "c_mask01T": _HC["mask01T"], "c_sel": _HC["sel"], "c_candmask": _HC["candmask"],
        "c_rt_cos": _HC["rt_cos"], "c_rt_sin": _HC["rt_sin"], "c_scl": _HC["scl"],
    }
    x = f(inputs["x"])
    mem = f(inputs["mem"])
    maps = []
    for b in range(NB):
        m = dict(shared)
        m["x"] = x[b]
        m["mem"] = mem[b]
        maps.append(m)
    return maps


def kernel(**inputs):
    maps = _prep_inputs(inputs)
    nc, _ = build_program()
    res = run_bass_kernel_spmd(nc, maps, core_ids=list(range(NB)))
    return np.stack([np.asarray(r["y"], dtype=np.float32) for r in res.results], axis=0)
```

```python
import contextlib
import math
import numpy as np
import concourse.bass as bass
import concourse.mybir as mybir
from concourse.ap import AP
from concourse.bass_utils import run_bass_kernel_spmd

F32 = mybir.dt.float32
BF16 = mybir.dt.bfloat16
AF = mybir.ActivationFunctionType
ALU = mybir.AluOpType
AX = mybir.AxisListType

NB = 8
S = 4096
D = 1024
T = 256
NT = S // T
DIN = 7168
DFF = 2816
MEM = 256
EPS = 1e-6
NEG = -30000.0
EPOCH = 2000
EMBED_WAIT = True
import os as _os
NOSEEN = bool(int(_os.environ.get("KNOSEEN", "0")))


def _rel_bucket_np(d):
    d = np.maximum(d, 0)
    df = np.maximum(d, 1).astype(np.float32)
    large = 16 + (np.log(df / np.float32(16)) / np.float32(math.log(2048 / 16)) * np.float32(16)).astype(np.int32)
    large = np.minimum(large, 31)
    return np.where(d < 16, d, large)


_bk = _rel_bucket_np(np.arange(0, 4096))
D31 = int(np.min(np.nonzero(_bk == 31)[0]))
assert np.all(_bk[D31:] == 31)
DELTA_FAR = ((D31 + 127 + 127) // 128) * 128
TABL = DELTA_FAR + 255 + 128 + 1


def _host_consts():
    c = {}
    c["ident"] = np.eye(128, dtype=np.float32)
    jj = np.arange(128)
    c["mask01T"] = (jj[:, None] <= jj[None, :]).astype(np.float32)
    sel = np.zeros((128, 16, 128), np.float32)
    for n in range(16):
        sel[n, n, :] = 1.0
    c["sel"] = sel.reshape(128, 16 * 128)
    cm = np.zeros((128, NT, 16), np.float32)
    for tt in range(NT):
        cm[:, tt, tt:] = -1e30
    c["candmask"] = cm.reshape(128, NT * 16)
    pos = np.arange(S, dtype=np.float64)
    half = 64
    inv = (10000.0 ** (-np.arange(half, dtype=np.float32) / np.float32(half))).astype(np.float32)
    ang = (pos.astype(np.float32)[:, None] * inv[None, :]).astype(np.float32).astype(np.float64)
    cos, sin = np.cos(ang), np.sin(ang)
    p = (np.arange(S) % 128).astype(np.float64)
    lg = np.log1p(-np.power(2.0, -5.0 - np.arange(4, dtype=np.float64)))
    up = np.exp(-lg[None, :] * (127.0 - p[:, None]))
    dn = np.exp(lg[None, :] * (127.0 - p[:, None])) / math.sqrt(128.0)
    c["rt_cos"] = cos.astype(np.float32)
    c["rt_sin"] = sin.astype(np.float32)
    c["scl"] = np.concatenate([up[:128], dn[:128]], axis=1).astype(np.float32)
    c["gamC"] = np.exp(lg * 128.0)
    return c


_HC = _host_consts()
GAMC = [float(v) for v in _HC["gamC"]]


import types as _types


def _freeze(fn):
    if fn is None or fn.__closure__ is None:
        return fn
    cells = []
    for c in fn.__closure__:
        try:
            cells.append(_types.CellType(c.cell_contents))
        except ValueError:
            cells.append(c)
    return _types.FunctionType(fn.__code__, fn.__globals__, fn.__name__, fn.__defaults__, tuple(cells))


class Tk:
    __slots__ = ("name", "w", "r", "alias")

    def __init__(self, name):
        self.name = name
        self.w = None
        self.r = {}
        self.alias = []


class Prog:
    ENG = ("pe", "act", "dve", "pool", "sp")

    def __init__(self):
        self.q = {e: [] for e in self.ENG}
        self.cnt = {e: 0 for e in self.ENG}
        self.seen = {e: {} for e in self.ENG}
        self.dcnt = {}

    def emit(self, eng, fn, reads=(), writes=(), dma=None):
        fn = _freeze(fn)
        deps = {}

        def add(tok):
            if tok is None:
                return
            k, v = tok
            if deps.get(k, 0) < v:
                deps[k] = v

        for t in reads:
            add(t.w)
            for a in t.alias:
                add(a.w)
            if t.name.startswith("ps"):
                for k, v in t.r.items():
                    if k != ("eng", eng):
                        add((k, v))
        for t in writes:
            add(t.w)
            for k, v in t.r.items():
                add((k, v))
            for a in t.alias:
                add(a.w)
                for k, v in a.r.items():
                    add((k, v))
        if dma is not None:
            add((("dma", dma), self.dcnt.get(dma, 0)))
        waits = []
        seen = self.seen[eng]
        for k, v in deps.items():
            if v <= 0:
                continue
            if k == ("eng", "pe") and eng == "pe":
                continue
            if seen.get(k, 0) >= v and not NOSEEN:
                continue
            seen[k] = v
            waits.append((k, v))
        if dma is None:
            self.cnt[eng] += 1
            tok = (("eng", eng), self.cnt[eng])
        else:
            self.dcnt[dma] = self.dcnt.get(dma, 0) + 16
            tok = (("dma", dma), self.dcnt[dma])
        self.q[eng].append((waits, fn, tok))
        for t in reads:
            if t.r.get(tok[0], 0) < tok[1]:
                t.r[tok[0]] = tok[1]
        for t in writes:
            t.w = tok
            t.r = {}
        return tok


def _b(ap, dims):
    return AP(ap.tensor, ap.offset, [list(ap.ap[0])] + [list(d) for d in dims])


def build_program(debug=None, ntiles=NT):
    nc = bass.Bass("TRN2", target_bir_lowering=False)
    P = Prog()

    def din(name, shape, dt=F32):
        return nc.dram_tensor(name, list(shape), dt, kind="ExternalInput").ap()

    x_d = din("x", [S, D])
    mem_d = din("mem", [MEM, D])
    w_in_d = din("w_in", [D, DIN])
    w_kv_d = din("w_mem_kv", [D, 1024])
    w_bra_d = din("w_br_attn", [512, D])
    w_brr_d = din("w_br_ret", [512, D])
    w_brm_d = din("w_br_mem", [512, D])
    w_out_d = din("w_out", [D, D])
    w_up_d = din("w_up", [D, 2 * DFF])
    w_dn_d = din("w_down", [DFF, D])
    gmix_d = din("gmix_col", [128, 8])
    gffn_d = din("gffn_col", [128, 8])
    gmem_d = din("gmem_col", [128, 8])
    gfin_d = din("g_final", [1, D])
    gn_d = din("ret_gn_gain", [1, 512])
    cw_d = din("cw_col", [128, 44 * 3])
    cb_d = din("cb_col", [128, 44])
    rb31_d = din("rb31", [1, 8])
    bsk_d = din("bias_skew", [8, 128, TABL])
    ident_d = din("c_ident", [128, 128])
    m01_d = din("c_mask01T", [128, 128])
    sel_d = din("c_sel", [128, 16 * 128])
    cand_d = din("c_candmask", [128, NT * 16])
    rt_d = [din("c_rt_" + k, [S, 64]) for k in ("cos", "sin")]
    scl_d = din("c_scl", [128, 8])
    y_d = nc.dram_tensor("y", [S, D], F32, kind="ExternalOutput").ap()
    dbg_d = nc.dram_tensor("dbg", [S, D], F32, kind="ExternalOutput").ap() if debug else None

    def dscr(name, shape):
        return nc.dram_tensor(name, list(shape), BF16, kind="Internal").ap()

    wb_in = dscr("wb_in", [D, DIN])
    wb_kv = dscr("wb_kv", [D, 1024])
    wb_bra = dscr("wb_bra", [512, D])
    wb_brr = dscr("wb_brr", [512, D])
    wb_brm = dscr("wb_brm", [512, D])
    wb_out = dscr("wb_out", [D, D])
    wb_up = dscr("wb_up", [D, 2 * DFF])
    wb_dn = dscr("wb_dn", [DFF, D])
    bsk_b = dscr("bsk_b", [8, 128, TABL])

    es = contextlib.ExitStack()
    with es:
        def sb(name, shape, dt):
            return es.enter_context(nc.sbuf_tensor(name, list(shape), dt))

        def sem(name):
            return es.enter_context(nc.semaphore(name))

        kpair = [sb(f"kpair{j}", [128, S], BF16) for j in range(4)]
        Va = sb("Va", [128, 32, 8, 65], BF16)
        xsb = [sb(f"xs{i}", [128, 2, D], F32) for i in range(2)]
        nT = sb("nT", [128, 8, T], BF16)
        wbuf = [sb(f"wbuf{i}", [128, 4096], BF16) for i in range(3)]
        qaug = [sb(f"qaug{h}", [128, T], BF16) for h in range(8)]
        maskT = [sb(f"maskT{h}", [128, T], BF16) for h in range(8)]
        maskpad = sb("maskpad", [128, 8, 128], BF16)
        sel = sb("sel", [128, 16, 128], BF16)
        kmeanT = [sb(f"kmeanT{j}", [128, 16], BF16) for j in range(4)]
        qhT = sb("qhT", [128, 4, T], BF16)
        khT = sb("khT", [128, 4, T], BF16)
        khtok = sb("khtok", [128, 2, 512], BF16)
        vrtok = sb("vrtok", [128, 2, 512], BF16)
        gs = sb("gs", [128, 2, 512], BF16)
        qmT = sb("qmT", [128, 4, T], BF16)
        tz = sb("tz", [128, 3, 8, T], BF16)
        yT = [sb(f"yT{i}", [128, 4, T], BF16) for i in range(3)]
        ytok1 = sb("ytok", [128, 2, 512], BF16)
        ytok = [ytok1, ytok1, ytok1]
        mergedT = sb("mergedT", [128, 8, T], BF16)
        btab = [sb(f"btab{i}", [128, TABL], BF16) for i in range(2)]
        rtab = [sb(f"rtab{i}", [128, 2, 64], F32) for i in range(2)]
        scl = sb("scl", [128, 8], F32)
        sbias = [sb(f"sbias{i}", [128, T], F32) for i in range(2)]
        pT = [sb(f"pT{i}", [128, T], BF16) for i in range(4)]
        KmT = sb("KmT", [128, 4, MEM], BF16)
        Vm = sb("Vm", [128, 2, 4, 129], BF16)
        aT = sb("aT", [128, 22, T], BF16)
        hbuf = [sb(f"hbuf{i}", [128, T + 2], F32) for i in range(2)]
        cacc = [sb(f"cacc{i}", [128, T], F32) for i in range(2)]
        carry = sb("carry", [128, 44, 2], F32)
        Sst = sb("Sst", [128, 4, 128], F32)
        Sbf = sb("Sbf", [128, 4, 128], BF16)
        AT = sb("AT", [128, 4, 128], BF16)
        f32a = sb("f32a", [128, 512], F32)
        f32b = sb("f32b", [128, 512], F32)
        ysq = f32a
        xrot = f32a
        yc = f32b
        tg = f32a
        ug = f32b
        xn = sb("xn", [128, 2, D], BF16)
        stat = sb("stat", [128, 32], F32)
        gsb = sb("gsb", [128, 8, 16], F32)
        top8 = sb("top8", [128, 8, 8], F32)
        rden = sb("rden", [128, 2], F32)
        ident = sb("ident", [128, 128], BF16)
        m01 = sb("m01", [128, 128], F32)
        cand = sb("cand", [128, 16], F32)
        gmix = sb("gmixc", [128, 8], F32)
        gffn = sb("gffnc", [128, 8], F32)
        gmem = sb("gmemc", [128, 8], F32)
        gnh = sb("gnh", [128, 512], F32)
        cw = sb("cw", [128, 44, 3], F32)
        cb = sb("cb", [128, 44], F32)
        rb31 = sb("rb31_sb", [128, 8], F32)
        memnT = aT

        rtmp = [aT[:, 0:2, :].rearrange("p a b -> p (a b)").bitcast(F32), aT[:, 2:4, :].rearrange("p a b -> p (a b)").bitcast(F32)]
        rottok = aT[:, 4:8, :].rearrange("p (s a) b -> p s (a b)", s=2)
        mtmp = f32a[:, 0:T]
        mtmp2 = f32b[:, 0:T]
        gact = f32a[:, T:2 * T]

        sb2 = [aT[:, 8:12, :].rearrange("p a b -> p (a b)").bitcast(F32), aT[:, 12:16, :].rearrange("p a b -> p (a b)").bitcast(F32)]
        pT2 = [sbias[0][:].bitcast(BF16), sbias[1][:].bitcast(BF16),
               aT[:, 16:18, :].rearrange("p a b -> p (a b)"), aT[:, 18:20, :].rearrange("p a b -> p (a b)")]
        pT2n = ["sbias0", "sbias1", "pT2_2", "pT2_3"]

        psF = [es.enter_context(nc.psum_tensor(f"psF{i}", [128, 512], F32)) for i in range(6)]
        psB = [es.enter_context(nc.psum_tensor(f"psB{i}", [128, 1024], BF16)) for i in range(2)]

        nep = {"pe": 18, "act": 8, "dve": 10, "pool": 6, "sp": 1}
        esem = {e: [sem(f"s_{e}{i}") for i in range(nep[e])] for e in Prog.ENG}
        dsem = {}

        def dma_sem(name):
            if name not in dsem:
                dsem[name] = sem("d_" + name)
            return name

        tk = {}

        def K(obj_name):
            if obj_name not in tk:
                tk[obj_name] = Tk(obj_name)
            return tk[obj_name]

        for _n in ("rtmp0", "rtmp1", "rottok", "sb2_0", "sb2_1", "pT2_2", "pT2_3"):
            K(_n).alias.append(K("aT"))
            K("aT").alias.append(K(_n))

        mute = [False]
        kcut = int(_os.environ.get("KCUT", "99"))

        def PE(fn, r=(), w=()):
            if mute[0]:
                return
            P.emit("pe", fn, [K(a) for a in r], [K(a) for a in w])

        def ACT(fn, r=(), w=()):
            if mute[0]:
                return
            P.emit("act", fn, [K(a) for a in r], [K(a) for a in w])

        def DVE(fn, r=(), w=()):
            if mute[0]:
                return
            P.emit("dve", fn, [K(a) for a in r], [K(a) for a in w])

        def POOL(fn, r=(), w=()):
            if mute[0]:
                return
            P.emit("pool", fn, [K(a) for a in r], [K(a) for a in w])

        def DMA(eng, semname, out, in_, r=(), w=()):
            if mute[0]:
                return
            dma_sem(semname)
            P.emit(eng, lambda e, out=out, in_=in_: e.dma_start(out=out, in_=in_),
                   [K(a) for a in r], [K(a) for a in w], dma=semname)

        mmrot = [0]

        def mmbank():
            i = mmrot[0] % 4
            mmrot[0] += 1
            return i

        trrot = [0]

        def trbank():
            i = trrot[0] % 2
            trrot[0] += 1
            return i

        wsrc = {"kv": w_kv_d, "in": w_in_d, "bra": w_bra_d, "brr": w_brr_d, "brm": w_brm_d, "out": w_out_d, "up": w_up_d, "dn": w_dn_d}
        conv_i = [0]
        for h in range(8):
            DMA("pool", f"cv{conv_i[0] % 4}", bsk_b[h], bsk_d[h], r=(), w=(f"cv_bsk{h}",))
            conv_i[0] += 1

        sdesc = []
        for si in range(2):
            sdesc.append(("kv", 0, 8, si * 512, 512))
        NSETUP = len(sdesc)
        for si in range(14):
            sdesc.append(("in", 0, 8, si * 512, 512))
        sdesc += [("bra", 0, 4, 0, 1024), ("brr", 0, 4, 0, 1024), ("brm", 0, 4, 0, 1024)]
        for ch in range(2):
            sdesc.append(("out", 0, 8, ch * 512, 512))
        for pg in range(6):
            ncol = 512 if pg < 5 else 256
            sdesc.append(("up", 0, 8, pg * 512, ncol))
            sdesc.append(("up", 0, 8, DFF + pg * 512, ncol))
        for ch in range(2):
            for kg in range(3):
                sdesc.append(("dn", kg * 8, 8 if kg < 2 else 6, ch * 512, 512))
        NDIST = len(sdesc)
        NPER = NDIST - NSETUP
        wsc = nc.dram_tensor("wsc", [NDIST, 128, 4096], BF16, kind="Internal").ap()
        conv_names = {}

        def ensure_conv(j):
            if j in conv_names:
                return conv_names[j]
            key, r0, nk, c0, ncol = sdesc[j]
            sv = wsrc[key].rearrange("(kc p) c -> p kc c", p=128)
            names = []
            for k0 in range(0, nk, 2):
                kk = min(2, nk - k0)
                name = f"cvs{j}_{k0}"
                DMA("pool", f"cv{conv_i[0] % 6}", wsc[j][:, k0 * ncol:(k0 + kk) * ncol].rearrange("p (k c) -> p k c", k=kk),
                    sv[:, r0 + k0:r0 + k0 + kk, c0:c0 + ncol], r=(), w=(name,))
                conv_i[0] += 1
                names.append(name)
            conv_names[j] = tuple(names)
            return conv_names[j]

        NSLAB = NSETUP + NPER * NT

        def distinct(i):
            return i if i < NSETUP else NSETUP + (i - NSETUP) % NPER

        ws = {"issued": 0, "used": 0}

        def issue_loads(upto):
            for i2 in range(ws["issued"], min(upto + 4, NDIST)):
                ensure_conv(i2)
            while ws["issued"] < min(upto, NSLAB):
                i = ws["issued"]
                j = distinct(i)
                key, r0, nk, c0, ncol = sdesc[j]
                slot = i % 3
                DMA("sp", f"wb{slot}", wbuf[slot][:, 0:nk * ncol], wsc[j][:, 0:nk * ncol], r=ensure_conv(j), w=(f"wbuf{slot}",))
                ws["issued"] += 1

        def next_slab(hold=0):
            i = ws["used"]
            issue_loads(i + 3 - hold)
            ws["used"] += 1
            key, r0, nk, c0, ncol = sdesc[distinct(i)]
            slot = i % 3
            return f"wbuf{slot}", wbuf[slot][:, 0:nk * ncol].rearrange("p (k c) -> p k c", k=nk)

        def ld(dst_ap, src_ap, name, eng="sp"):
            DMA(eng, "c_" + name, dst_ap, src_ap, w=(name,))

        ld(ident[:], ident_d[:, :], "ident", eng="pool")
        ld(m01[:], m01_d[:, :], "m01")
        ld(sel[:].rearrange("p a b -> p (a b)"), sel_d[:, :], "sel", eng="pool")
        ld(scl[:], scl_d[:, :], "scl")
        ld(gmix[:], gmix_d[:, :], "gmix")
        ld(gffn[:], gffn_d[:, :], "gffn")
        ld(gmem[:], gmem_d[:, :], "gmem")
        ld(cw[:].rearrange("p a b -> p (a b)"), cw_d[:, :], "cw")
        ld(cb[:], cb_d[:, :], "cb")
        ld(gnh[:], AP(gn_d.tensor, 0, [[0, 128], [1, 512]]), "gnh")
        ld(rb31[:], AP(rb31_d.tensor, 0, [[0, 128], [1, 8]]), "rb31")
        ld(xsb[0][:], mem_d.rearrange("(s p) d -> p s d", p=128), "xs0")

        POOL(lambda e: e.tensor_scalar(out=gnh[:], in0=gnh[:], scalar1=0.5, scalar2=None, op0=ALU.mult), r=("gnh",), w=("gnh",))
        POOL(lambda e: e.memset(carry[:].rearrange("p a b -> p (a b)"), 0.0), w=("carry",))
        POOL(lambda e: e.memset(Sst[:].rearrange("p a b -> p (a b)"), 0.0), w=("Sst",))
        POOL(lambda e: e.memset(Sbf[:].rearrange("p a b -> p (a b)"), 0.0), w=("Sbf",))
        POOL(lambda e: e.memset(Va[:].rearrange("p a b c -> p (a b c)"), 1.0), w=("Va",))
        POOL(lambda e: e.memset(Vm[:].rearrange("p a b c -> p (a b c)"), 1.0), w=("Vm",))
        for h in range(8):
            POOL(lambda e, h=h: e.memset(qaug[h][:], 0.0), w=(f"qaug{h}",))
        for j in range(4):
            POOL(lambda e, j=j: e.memset(kmeanT[j][:], 0.0), w=(f"kmeanT{j}",))
        POOL(lambda e: e.memset(maskpad[:].rearrange("p a b -> p (a b)"), 0.0), w=("maskpad",))

        def rms_stats(src_name, src, nsub, width):
            for s_ in range(nsub):
                ACT(lambda e, s_=s_: e.activation(out=xn[:, s_, 0:width], in_=src[:, s_, :], func=AF.Square,
                                                  accum_out=stat[:, s_:s_ + 1]),
                    r=(src_name,), w=(f"xn{s_}", f"stat{s_}"))
            names = tuple(f"stat{s_}" for s_ in range(nsub))
            DVE(lambda e: e.tensor_scalar(out=stat[:, 4:4 + nsub], in0=stat[:, 0:nsub], scalar1=1.0 / width,
                                          scalar2=EPS, op0=ALU.mult, op1=ALU.add), r=names, w=("stat_v",))
            ACT(lambda e: e.activation(out=stat[:, 4:4 + nsub], in_=stat[:, 4:4 + nsub], func=AF.Sqrt),
                r=("stat_v",), w=("stat_v",))
            DVE(lambda e: e.reciprocal(out=stat[:, 8:8 + nsub], in_=stat[:, 4:4 + nsub]), r=("stat_v",), w=("stat_r",))

        def norm_p1(src_name, src, ntok):
            nsub = ntok // 128
            rms_stats(src_name, src, nsub, D)
            for s_ in range(nsub):
                DVE(lambda e, s_=s_: e.tensor_scalar(out=xn[:, s_, :], in0=src[:, s_, :], scalar1=stat[:, 8 + s_:9 + s_],
                                                     scalar2=None, op0=ALU.mult), r=(src_name, "stat_r"), w=(f"xn{s_}",))

        def norm_to_T(src_name, src, gcol, gname, dst, dst_name, ntok):
            norm_p1(src_name, src, ntok)
            norm_p2(gcol, gname, dst, dst_name, ntok)

        def norm_p2(gcol, gname, dst, dst_name, ntok):
            nsub = ntok // 128
            for kc in range(8):
                b = trbank()
                for s_ in range(nsub):
                    PE(lambda e, kc=kc, s_=s_, b=b: e.transpose(out=psB[b][:, s_ * 128:(s_ + 1) * 128],
                                                                in_=xn[:, s_, kc * 128:(kc + 1) * 128], identity=ident[:]),
                       r=(f"xn{s_}", "ident"), w=(f"psB{b}",))
                ACT(lambda e, kc=kc, b=b: e.activation(out=dst[:, kc, 0:ntok], in_=psB[b][:, 0:ntok], func=AF.Identity,
                                                        scale=gcol[:, kc:kc + 1]), r=(f"psB{b}", gname), w=(dst_name,))

        def mm_fm(wname, wv, c0, rhs_t, rhs_name, nk, bank, ntok=T):
            for kc in range(nk):
                PE(lambda e, kc=kc: e.matmul(psF[bank][:, 0:ntok], lhsT=wv[:, kc, c0:c0 + 128], rhs=rhs_t[:, kc, 0:ntok],
                                             start=(kc == 0), stop=(kc == nk - 1)),
                   r=(wname, rhs_name), w=(f"psF{bank}",))

        def mm_tm(wname, wv, lhs_t, lhs_name, s_, nk, bank, ncol=512, kofs=0, first=True, last=True):
            for kc in range(nk):
                PE(lambda e, kc=kc: e.matmul(psF[bank][:, 0:ncol], lhsT=lhs_t[:, kofs + kc, s_ * 128:(s_ + 1) * 128],
                                             rhs=wv[:, kc, 0:ncol], start=(first and kc == 0), stop=(last and kc == nk - 1)),
                   r=(wname, lhs_name), w=(f"psF{bank}",))

        def transpose_tok(src, src_name, dst, dst_name, nchunk):
            for j in range(nchunk):
                b = trbank()
                for s_ in range(2):
                    PE(lambda e, j=j, s_=s_, b=b: e.transpose(out=psB[b][:, s_ * 128:(s_ + 1) * 128],
                                                              in_=src[:, s_, j * 128:(j + 1) * 128], identity=ident[:]),
                       r=(src_name, "ident"), w=(f"psB{b}",))
                DVE(lambda e, j=j, b=b: e.tensor_copy(out=dst[:, j, :], in_=psB[b][:, 0:T]), r=(f"psB{b}",), w=(dst_name,))

        norm_to_T("xs0", xsb[0], gmem, "gmem", memnT, "aT", MEM)
        wn, wv = next_slab()
        for h in range(4):
            bk = mmbank()
            mm_fm(wn, wv, h * 128, memnT, "aT", 8, bk, ntok=MEM)
            ACT(lambda e, h=h, bk=bk: e.activation(out=KmT[:, h, :], in_=psF[bk][:, 0:MEM], func=AF.Identity),
                r=(f"psF{bk}",), w=("KmT",))
        wn, wv = next_slab()
        for mc in range(2):
            bk = mmbank()
            mm_tm(wn, wv, memnT, "aT", mc, 8, bk)
            ACT(lambda e, mc=mc, bk=bk: e.activation(out=Vm[:, mc, :, 0:128],
                                                      in_=psF[bk][:, 0:512].rearrange("p (h e) -> p h e", h=4),
                                                      func=AF.Identity), r=(f"psF{bk}",), w=("Vm",))

        LA = 3

        def dbg_store(tag, src_ap, ncol, eng, t0):
            if debug == tag:
                DMA(eng, "dbg", dbg_d[t0:t0 + T, 0:ncol].rearrange("(s p) d -> p s d", p=128), src_ap, r=("ytok", "xs0", "xs1"), w=("dbgdram",))

        _tl = [int(v) for v in _os.environ["KTILES"].split(",")] if _os.environ.get("KTILES") else list(range(ntiles))
        def barrier():
            allt = list(tk.values())
            P.emit("act", lambda e: e.activation(out=stat[:, 30:31], in_=stat[:, 30:31], func=AF.Identity), [], allt)
            P.emit("dve", lambda e: e.memset(stat[:, 31:32], 0.0), [], allt)
            P.emit("pool", lambda e: e.memset(stat[:, 29:30], 0.0), [], allt)
            P.emit("sp", None, [], allt)

        for _ti, tt in enumerate(_tl):
            t0 = tt * T
            if _ti > 0 and _os.environ.get("KBAR"):
                barrier()

            def mark(k, _ti=_ti):
                if _ti > 0 and k >= kcut:
                    mute[0] = True
            X = xsb[tt % 2]
            Xn = f"xs{tt % 2}"
            nxt = _tl[_ti + 1] if _ti + 1 < len(_tl) else None
            if _ti == 0:
                DMA("sp", f"xld{tt % 2}", X[:], x_d[t0:t0 + T, :].rearrange("(s p) d -> p s d", p=128), w=(Xn,))
                norm_to_T(Xn, X, gmix, "gmix", nT, "nT", T)
            if nxt is not None:
                Xq, Xqn = xsb[nxt % 2], f"xs{nxt % 2}"
            if tt > 3:
                DMA("sp", "cand", cand[:], cand_d[:, tt * 16:(tt + 1) * 16], w=("cand",))
            for i in range(2):
                DMA("sp", f"rt{i}", rtab[i][:], rt_d[i][t0:t0 + T, :].rearrange("(s p) c -> p s c", p=128), w=(f"rtab{i}",))
            mark(0)

            mark(1)
            wn, wv = next_slab()
            for j in range(4):
                bk = mmbank()
                mm_fm(wn, wv, j * 128, nT, "nT", 8, bk)
                ACT(lambda e, j=j, bk=bk: e.activation(out=qaug[2 * j][0:64, :], in_=psF[bk][0:64, 0:T], func=AF.Identity,
                                                        scale=0.125), r=(f"psF{bk}",), w=(f"qaug{2 * j}",))
                DVE(lambda e, j=j, bk=bk: e.tensor_scalar(out=qaug[2 * j + 1][64:128, :], in0=psF[bk][64:128, 0:T], scalar1=0.125,
                                                           scalar2=None, op0=ALU.mult), r=(f"psF{bk}",), w=(f"qaug{2 * j + 1}",))
            wn, wv = next_slab()
            for j in range(4):
                bk = mmbank()
                mm_fm(wn, wv, j * 128, nT, "nT", 8, bk)
                ACT(lambda e, j=j, bk=bk: e.activation(out=kpair[j][:, t0:t0 + T], in_=psF[bk][:, 0:T], func=AF.Identity),
                    r=(f"psF{bk}",), w=(f"kpair{j}",))
                DVE(lambda e, bk=bk: e.tensor_reduce(out=stat[:, 12:13], in_=psF[bk][:, 0:T], axis=AX.X, op=ALU.add),
                    r=(f"psF{bk}",), w=("stat_k",))
                DVE(lambda e, j=j: e.tensor_scalar(out=kmeanT[j][:, tt:tt + 1], in0=stat[:, 12:13], scalar1=1.0 / T, scalar2=None,
                                                    op0=ALU.mult), r=("stat_k",), w=(f"kmeanT{j}",))
            wn, wv = next_slab()
            for s_ in range(2):
                bk = mmbank()
                mm_tm(wn, wv, nT, "nT", s_, 8, bk)
                ACT(lambda e, s_=s_, bk=bk: e.activation(out=Va[:, 2 * tt + s_, :, 0:64],
                                                          in_=psF[bk][:, 0:512].rearrange("p (h e) -> p h e", h=8),
                                                          func=AF.Identity), r=(f"psF{bk}",), w=("Va",))
            def gate_p1(s_):
                bk = mmbank()
                for h in range(8):
                    PE(lambda e, h=h: e.matmul(psF[bk][:, h * 16:(h + 1) * 16], lhsT=qaug[h][:, s_ * 128:(s_ + 1) * 128],
                                               rhs=kmeanT[h // 2][:, 0:16], start=True, stop=True),
                       r=(f"qaug{h}", f"kmeanT{h // 2}"), w=(f"psF{bk}",))
                cmv = _b(cand[:, :], [[0, 8], [1, 16]])
                DVE(lambda e: e.tensor_tensor(out=gsb[:], in0=psF[bk][:, 0:128].rearrange("p (h n) -> p h n", h=8),
                                              in1=cmv, op=ALU.add), r=(f"psF{bk}", "cand"), w=("gsb",))
                for h in range(8):
                    DVE(lambda e, h=h: e.max(out=top8[:, h, :], in_=gsb[:, h, :]), r=("gsb",), w=("top8",))
                for h in range(8):
                    DVE(lambda e, h=h: e.tensor_scalar(out=maskpad[:, h, 0:16], in0=gsb[:, h, :], scalar1=top8[:, h, 2:3],
                                                       scalar2=None, op0=ALU.is_ge), r=("gsb", "top8"), w=("maskpad",))
                DVE(lambda e: e.tensor_scalar(out=maskpad[:, :, 0:16], in0=maskpad[:, :, 0:16], scalar1=-1.0,
                                              scalar2=-NEG, op0=ALU.add, op1=ALU.mult), r=("maskpad",), w=("maskpad",))

            def gate_p2(s_):
                for h in range(8):
                    b = trbank()
                    PE(lambda e, h=h: e.transpose(out=psB[b][:, 0:128], in_=maskpad[:, h, :], identity=ident[:]),
                       r=("maskpad", "ident"), w=(f"psB{b}",))
                    ACT(lambda e, h=h: e.activation(out=maskT[h][:, s_ * 128:(s_ + 1) * 128], in_=psB[b][:, 0:128], func=AF.Identity),
                        r=(f"psB{b}",), w=(f"maskT{h}",))

            def ret_a(s_):
                bk = mmbank()
                for h in range(4):
                    PE(lambda e, h=h, s_=s_, bk=bk: e.matmul(psF[bk][:, h * 128:(h + 1) * 128], lhsT=khT[:, h, s_ * 128:(s_ + 1) * 128],
                                                              rhs=qhT[:, h, s_ * 128:(s_ + 1) * 128], start=True, stop=True),
                       r=("khT", "qhT"), w=(f"psF{bk}",))
                DVE(lambda e, bk=bk: e.tensor_tensor(out=AT[:], in0=psF[bk][:, 0:512].rearrange("p (h c) -> p h c", h=4),
                                                     in1=_b(m01[:], [[0, 4], [1, 128]]), op=ALU.mult), r=(f"psF{bk}", "m01"), w=("AT",))
                return bk

            def ret_b(s_):
                for h in range(4):
                    PE(lambda e, h=h, s_=s_: e.matmul(psF[4][:, h * 128:(h + 1) * 128], lhsT=AT[:, h, :], rhs=vrtok[:, s_, h * 128:(h + 1) * 128],
                                                       start=True, stop=False), r=("AT", "vrtok"), w=("psF4",))
                    PE(lambda e, h=h, s_=s_: e.matmul(psF[4][:, h * 128:(h + 1) * 128], lhsT=qhT[:, h, s_ * 128:(s_ + 1) * 128], rhs=Sbf[:, h, :],
                                                       start=False, stop=True), r=("qhT", "Sbf"), w=("psF4",))
                for h in range(4):
                    PE(lambda e, h=h, s_=s_: e.matmul(psF[5][:, h * 128:(h + 1) * 128], lhsT=khtok[:, s_, h * 128:(h + 1) * 128],
                                                       rhs=vrtok[:, s_, h * 128:(h + 1) * 128], start=True, stop=True),
                       r=("khtok", "vrtok"), w=("psF5",))
                for h in range(4):
                    DVE(lambda e, h=h: e.scalar_tensor_tensor(out=Sst[:, h, :], in0=Sst[:, h, :], scalar=GAMC[h], in1=psF[5][:, h * 128:(h + 1) * 128],
                                                              op0=ALU.mult, op1=ALU.add), r=("Sst", "psF5"), w=("Sst",))
                for h in range(4):
                    POOL(lambda e, h=h: e.tensor_scalar(out=Sbf[:, h, :], in0=Sst[:, h, :], scalar1=GAMC[h], scalar2=None, op0=ALU.mult),
                         r=("Sst",), w=("Sbf",))
                pyv = psF[4][:, 0:512].rearrange("p (h e) -> p h e", h=4)
                DVE(lambda e, pyv=pyv: e.tensor_reduce(out=stat[:, 16:20], in_=pyv, axis=AX.X, op=ALU.add), r=("psF4",), w=("gn_s",))
                ACT(lambda e: e.activation(out=ysq[:], in_=psF[4][:, 0:512], func=AF.Square), r=("psF4",), w=("f32a",))
                DVE(lambda e: e.tensor_reduce(out=stat[:, 20:24], in_=ysq[:].rearrange("p (h e) -> p h e", h=4), axis=AX.X, op=ALU.add),
                    r=("f32a",), w=("gn_q",))
                DVE(lambda e: e.tensor_scalar(out=stat[:, 16:20], in0=stat[:, 16:20], scalar1=1.0 / 128, scalar2=None, op0=ALU.mult),
                    r=("gn_s",), w=("gn_s",))
                DVE(lambda e: e.tensor_tensor(out=stat[:, 24:28], in0=stat[:, 16:20], in1=stat[:, 16:20], op=ALU.mult), r=("gn_s",), w=("gn_m2",))
                DVE(lambda e: e.scalar_tensor_tensor(out=stat[:, 20:24], in0=stat[:, 20:24], scalar=1.0 / 128, in1=stat[:, 24:28],
                                                     op0=ALU.mult, op1=ALU.subtract), r=("gn_q", "gn_m2"), w=("gn_q",))
                DVE(lambda e: e.tensor_scalar(out=stat[:, 20:24], in0=stat[:, 20:24], scalar1=EPS, scalar2=None, op0=ALU.add),
                    r=("gn_q",), w=("gn_q",))
                ACT(lambda e: e.activation(out=stat[:, 20:24], in_=stat[:, 20:24], func=AF.Sqrt), r=("gn_q",), w=("gn_q",))
                DVE(lambda e: e.reciprocal(out=stat[:, 24:28], in_=stat[:, 20:24]), r=("gn_q", "gn_m2"), w=("gn_m2",))
                ycv = yc[:].rearrange("p (h e) -> p h e", h=4)
                DVE(lambda e, pyv=pyv, ycv=ycv: e.tensor_tensor(out=ycv, in0=pyv, in1=_b(stat[:, 16:20], [[1, 4], [0, 128]]), op=ALU.subtract),
                    r=("psF4", "gn_s"), w=("f32b",))
                POOL(lambda e, ycv=ycv: e.tensor_tensor(out=ycv, in0=ycv, in1=_b(stat[:, 24:28], [[1, 4], [0, 128]]), op=ALU.mult),
                     r=("f32b", "gn_m2"), w=("f32b",))
                POOL(lambda e, s_=s_: e.tensor_tensor(out=ytok[1][:, s_, :], in0=yc[:], in1=gs[:, s_, :], op=ALU.mult),
                     r=("f32b", "gs"), w=("ytok",))


            def mem_attn():
                mitems = [(h, mc) for h in range(4) for mc in range(2)]

                def mem_s1(h, mc, idx):
                    bk = mmbank()
                    PE(lambda e: e.matmul(psF[bk][:, 0:T], lhsT=KmT[:, h, mc * 128:(mc + 1) * 128], rhs=qmT[:, h, :], start=True, stop=True),
                       r=("KmT", "qmT"), w=(f"psF{bk}",))
                    pi = idx % 4
                    ACT(lambda e: e.activation(out=pT[pi][:], in_=psF[bk][:, 0:T], func=AF.Exp, scale=128.0 ** -0.5),
                        r=(f"psF{bk}",), w=(f"pT{pi}",))

                def mem_s2(h, mc, idx):
                    acc = 4 + (h % 2)
                    pi = idx % 4
                    for s_ in range(2):
                        PE(lambda e, s_=s_: e.matmul(psF[acc][:, s_ * 129:(s_ + 1) * 129], lhsT=pT[pi][:, s_ * 128:(s_ + 1) * 128], rhs=Vm[:, mc, h, :],
                                                     start=(mc == 0 and s_ == 0), stop=(mc == 1), skip_group_check=True),
                           r=(f"pT{pi}", "Vm"), w=(f"psF{acc}",))
                    if mc == 1:
                        pov = psF[acc][:, 0:258].rearrange("p (s e) -> p s e", s=2)
                        DVE(lambda e: e.reciprocal(out=rden[:], in_=pov[:, :, 128]), r=(f"psF{acc}",), w=("rden",))
                        DVE(lambda e: e.tensor_tensor(out=ytok[2][:, :, h * 128:(h + 1) * 128], in0=pov[:, :, 0:128],
                                                      in1=_b(rden[:], [[1, 2], [0, 128]]), op=ALU.mult),
                            r=(f"psF{acc}", "rden"), w=("ytok",))

                for i in range(len(mitems) + LA):
                    if i < len(mitems):
                        mem_s1(mitems[i][0], mitems[i][1], i)
                    if i >= LA:
                        mem_s2(mitems[i - LA][0], mitems[i - LA][1], i - LA)


            def rot_transposes(which):
                src_t = rottok if which == 0 else khtok
                src_name = "rottok" if which == 0 else "khtok"
                dstT = qhT if which == 0 else khT
                dstT_name = "qhT" if which == 0 else "khT"
                for h in range(4):
                    b = trbank()
                    for s_ in range(2):
                        PE(lambda e, s_=s_: e.transpose(out=psB[b][:, s_ * 128:(s_ + 1) * 128], in_=src_t[:, s_, h * 128:(h + 1) * 128],
                                                        identity=ident[:]), r=(src_name, "ident"), w=(f"psB{b}",))
                    ACT(lambda e: e.activation(out=dstT[:, h, :], in_=psB[b][:, 0:T], func=AF.Identity), r=(f"psB{b}",), w=(dstT_name,))

            if tt > 3:
                gate_p1(0)
            for which in range(2):
                wn, wv = next_slab()
                for s_ in range(2):
                    bk = mmbank()
                    mm_tm(wn, wv, nT, "nT", s_, 8, bk)
                    DVE(lambda e, bk=bk, which=which: e.tensor_tensor(out=xrot[:].rearrange("p (h d) -> p h d", h=4),
                                                                      in0=psF[bk][:, 0:512].rearrange("p (h d) -> p h d", h=4),
                                                                      in1=_b(scl[:, which * 4:which * 4 + 4], [[1, 4], [0, 128]]), op=ALU.mult),
                        r=(f"psF{bk}", "scl"), w=("f32a",))
                    xv = xrot[:].rearrange("p (h t i) -> p h t i", h=4, t=2)
                    Ct = _b(rtab[0][:, s_, :], [[0, 4], [1, 64]])
                    St = _b(rtab[1][:, s_, :], [[0, 4], [1, 64]])
                    r4 = [rtmp[i][:].rearrange("p (h i) -> p h i", h=4) for i in range(2)]
                    dst_tok = khtok[:, s_, :] if which == 1 else rottok[:, s_, :]
                    dst_name = "khtok" if which == 1 else "rottok"
                    ov = dst_tok.rearrange("p (h t i) -> p h t i", h=4, t=2)
                    DVE(lambda e, xv=xv, Ct=Ct, r4=r4: e.tensor_tensor(out=r4[0], in0=xv[:, :, 0, :], in1=Ct, op=ALU.mult),
                        r=("f32a", "rtab0"), w=("rtmp0",))
                    POOL(lambda e, xv=xv, St=St, r4=r4: e.tensor_tensor(out=r4[1], in0=xv[:, :, 1, :], in1=St, op=ALU.mult),
                         r=("f32a", "rtab1"), w=("rtmp1",))
                    DVE(lambda e, ov=ov, r4=r4: e.tensor_tensor(out=ov[:, :, 0, :], in0=r4[0], in1=r4[1], op=ALU.subtract),
                        r=("rtmp0", "rtmp1"), w=(dst_name,))
                    POOL(lambda e, xv=xv, St=St, r4=r4: e.tensor_tensor(out=r4[0], in0=xv[:, :, 0, :], in1=St, op=ALU.mult),
                         r=("f32a", "rtab1"), w=("rtmp0",))
                    DVE(lambda e, xv=xv, Ct=Ct, r4=r4: e.tensor_tensor(out=r4[1], in0=xv[:, :, 1, :], in1=Ct, op=ALU.mult),
                        r=("f32a", "rtab0"), w=("rtmp1",))
                    POOL(lambda e, ov=ov, r4=r4: e.tensor_tensor(out=ov[:, :, 1, :], in0=r4[0], in1=r4[1], op=ALU.add),
                         r=("rtmp0", "rtmp1"), w=(dst_name,))
                if tt > 3:
                    if which == 0:
                        gate_p2(0)
                        gate_p1(1)
                    else:
                        gate_p2(1)
                if which == 1:
                    rot_transposes(0)
            wn, wv = next_slab()
            for s_ in range(2):
                bk = mmbank()
                mm_tm(wn, wv, nT, "nT", s_, 8, bk)
                ACT(lambda e, s_=s_, bk=bk: e.activation(out=vrtok[:, s_, :], in_=psF[bk][:, 0:512], func=AF.Identity),
                    r=(f"psF{bk}",), w=("vrtok",))
            rot_transposes(1)
            wn, wv = next_slab()
            for s_ in range(2):
                bk = mmbank()
                mm_tm(wn, wv, nT, "nT", s_, 8, bk)
                ACT(lambda e, bk=bk: e.activation(out=tg[:], in_=psF[bk][:, 0:512], func=AF.Tanh, scale=0.5),
                    r=(f"psF{bk}",), w=("f32a",))
                DVE(lambda e, bk=bk: e.scalar_tensor_tensor(out=ug[:], in0=tg[:], scalar=1.0, in1=psF[bk][:, 0:512],
                                                            op0=ALU.add, op1=ALU.mult), r=("f32a", f"psF{bk}"), w=("f32b",))
                POOL(lambda e, s_=s_: e.tensor_tensor(out=gs[:, s_, :], in0=ug[:], in1=gnh[:], op=ALU.mult),
                     r=("f32b", "gnh"), w=("gs",))
            wn, wv = next_slab()
            for h in range(4):
                bk = mmbank()
                mm_fm(wn, wv, h * 128, nT, "nT", 8, bk)
                ACT(lambda e, h=h, bk=bk: e.activation(out=qmT[:, h, :], in_=psF[bk][:, 0:T], func=AF.Identity),
                    r=(f"psF{bk}",), w=("qmT",))
            ret_a(0)
            for zi in range(3):
                for half in range(2):
                    wn, wv = next_slab()
                    for j in range(4):
                        bk = mmbank()
                        mm_fm(wn, wv, j * 128, nT, "nT", 8, bk)
                        ACT(lambda e, zi=zi, fc=half * 4 + j, bk=bk: e.activation(out=tz[:, zi, fc, :], in_=psF[bk][:, 0:T],
                                                                                func=AF.Tanh, scale=0.5),
                            r=(f"psF{bk}",), w=("tz",))
                    zs = zi * 2 + half
                    if zs == 0:
                        ret_b(0)
                    elif zs == 1:
                        ret_a(1)
                    elif zs == 2:
                        ret_b(1)
                    elif zs == 3:
                        transpose_tok(ytok[1], "ytok", yT[1], "yT1", 4)
                        dbg_store("yr", ytok[1][:], 512, "pool", t0)
                    elif zs == 4:
                        mem_attn()
                    else:
                        transpose_tok(ytok[2], "ytok", yT[2], "yT2", 4)
                        dbg_store("ym", ytok[2][:], 512, "pool", t0)

            mark(2)
            nblk = tt + 1
            items = [(h, b_) for h in range(8) for b_ in range(nblk)]

            def moba_s1(h, blk, idx):
                bb = h % 2
                if blk == 0:
                    if h == 0:
                        DMA("sp", "bt0", btab[0][:], bsk_b[0], r=("cv_bsk0",), w=("btab0",))
                    if h + 1 < 8:
                        DMA("sp", f"bt{(h + 1) % 2}", btab[(h + 1) % 2][:], bsk_b[h + 1], r=(f"cv_bsk{h + 1}",), w=(f"btab{(h + 1) % 2}",))
                use_mask = (blk < tt) and tt > 3
                bk = mmbank()
                for half, k0 in ((0, blk * 256 + 128), (1, blk * 256)):
                    PE(lambda e, half=half, k0=k0: e.matmul(psF[bk][:, half * T:(half + 1) * T], lhsT=kpair[h // 2][:, k0:k0 + 128], rhs=qaug[h][:],
                                                            start=True, stop=not use_mask, skip_group_check=True),
                       r=(f"kpair{h // 2}", f"qaug{h}"), w=(f"psF{bk}",))
                    if use_mask:
                        PE(lambda e, half=half: e.matmul(psF[bk][:, half * T:(half + 1) * T], lhsT=sel[:, blk, :], rhs=maskT[h][:],
                                                         start=False, stop=True, skip_group_check=True),
                           r=("sel", f"maskT{h}"), w=(f"psF{bk}",))
                pi = idx % 4
                dmin = t0 - (blk * 256 + 128)
                if dmin >= DELTA_FAR:
                    ACT(lambda e: e.activation(out=pT2[pi], in_=psF[bk][:, 0:2 * T], func=AF.Exp, bias=rb31[:, h:h + 1]),
                        r=(f"psF{bk}", "rb31"), w=(pT2n[pi],))
                else:
                    j0b = dmin + 128
                    si2 = idx % 2
                    DVE(lambda e: e.tensor_tensor(out=sb2[si2].rearrange("p (a b) -> p a b", a=2),
                                                  in0=psF[bk][:, 0:2 * T].rearrange("p (a b) -> p a b", a=2),
                                                  in1=_b(btab[bb][:, j0b:j0b + T], [[128, 2], [1, T]]), op=ALU.add),
                        r=(f"psF{bk}", f"btab{bb}"), w=(f"sb2_{si2}",))
                    ACT(lambda e: e.activation(out=pT2[pi], in_=sb2[si2], func=AF.Exp), r=(f"sb2_{si2}",), w=(pT2n[pi],))

            def moba_s2(h, blk, idx):
                acc = 4 + (h % 2)
                pi = idx % 4
                for half, c in ((0, 2 * blk + 1), (1, 2 * blk)):
                    for s_ in range(2):
                        PE(lambda e, s_=s_, half=half, c=c: e.matmul(psF[acc][:, s_ * 65:(s_ + 1) * 65],
                                                                     lhsT=pT2[pi][:, half * T + s_ * 128:half * T + (s_ + 1) * 128], rhs=Va[:, c, h, :],
                                                                     start=(blk == 0 and half == 0 and s_ == 0), stop=(blk == nblk - 1 and half == 1),
                                                                     skip_group_check=True),
                           r=(pT2n[pi], "Va"), w=(f"psF{acc}",))
                if blk == nblk - 1:
                    pov = psF[acc][:, 0:130].rearrange("p (s e) -> p s e", s=2)
                    DVE(lambda e: e.reciprocal(out=rden[:], in_=pov[:, :, 64]), r=(f"psF{acc}",), w=("rden",))
                    DVE(lambda e: e.tensor_tensor(out=ytok[0][:, :, h * 64:(h + 1) * 64], in0=pov[:, :, 0:64],
                                                  in1=_b(rden[:], [[1, 2], [0, 64]]), op=ALU.mult),
                        r=(f"psF{acc}", "rden"), w=("ytok",))

            def ya_pair_T(j):
                b = trbank()
                for s_ in range(2):
                    PE(lambda e, s_=s_: e.transpose(out=psB[b][:, s_ * 128:(s_ + 1) * 128], in_=ytok[0][:, s_, j * 128:(j + 1) * 128],
                                                    identity=ident[:]), r=("ytok", "ident"), w=(f"psB{b}",))
                DVE(lambda e: e.tensor_copy(out=yT[0][:, j, :], in_=psB[b][:, 0:T]), r=(f"psB{b}",), w=("yT0",))

            pend = []
            for i in range(len(items) + LA):
                if i < len(items):
                    moba_s1(items[i][0], items[i][1], i)
                if i >= LA:
                    hh, cc = items[i - LA]
                    moba_s2(hh, cc, i - LA)
                    if cc == nblk - 1 and hh % 2 == 1:
                        pend.append((i + max(2, nblk // 2), hh // 2))
                while pend and pend[0][0] <= i:
                    ya_pair_T(pend.pop(0)[1])
            for _, j in pend:
                ya_pair_T(j)
            dbg_store("ya", ytok[0][:], 512, "pool", t0)

            mark(5)
            if nxt is not None:
                DMA("sp", f"xld{nxt % 2}", Xq[:], x_d[nxt * T:(nxt + 1) * T, :].rearrange("(s p) d -> p s d", p=128), w=(Xqn,))
            wsl = [next_slab(hold=bi) for bi in range(3)]
            for fc in range(8):
                bks = []
                for bi in range(3):
                    bk = mmbank()
                    bks.append(bk)
                    mm_fm(wsl[bi][0], wsl[bi][1], fc * 128, yT[bi], f"yT{bi}", 4, bk)
                DVE(lambda e, fc=fc, bk=bks[0]: e.scalar_tensor_tensor(out=mtmp[:], in0=tz[:, 0, fc, :], scalar=1.0, in1=psF[bk][:, 0:T],
                                                                       op0=ALU.add, op1=ALU.mult), r=("tz", f"psF{bks[0]}"), w=("f32a",))
                DVE(lambda e, fc=fc, bk=bks[1]: e.scalar_tensor_tensor(out=mtmp2[:], in0=tz[:, 1, fc, :], scalar=1.0, in1=psF[bk][:, 0:T],
                                                                       op0=ALU.add, op1=ALU.mult), r=("tz", f"psF{bks[1]}"), w=("f32b",))
                POOL(lambda e: e.tensor_tensor(out=mtmp[:], in0=mtmp[:], in1=mtmp2[:], op=ALU.add), r=("f32a", "f32b"), w=("f32a",))
                DVE(lambda e, fc=fc, bk=bks[2]: e.scalar_tensor_tensor(out=mtmp2[:], in0=tz[:, 2, fc, :], scalar=1.0, in1=psF[bk][:, 0:T],
                                                                       op0=ALU.add, op1=ALU.mult), r=("tz", f"psF{bks[2]}"), w=("f32b",))
                POOL(lambda e, fc=fc: e.tensor_tensor(out=mergedT[:, fc, :], in0=mtmp[:], in1=mtmp2[:], op=ALU.add),
                     r=("f32a", "f32b"), w=("mergedT",))
            for ch in range(2):
                wn, wv = next_slab()
                for s_ in range(2):
                    bk = mmbank()
                    mm_tm(wn, wv, mergedT, "mergedT", s_, 8, bk)
                    DVE(lambda e, s_=s_, ch=ch, bk=bk: e.scalar_tensor_tensor(out=X[:, s_, ch * 512:(ch + 1) * 512], in0=psF[bk][:, 0:512], scalar=0.5,
                                                                              in1=X[:, s_, ch * 512:(ch + 1) * 512], op0=ALU.mult, op1=ALU.add),
                        r=(f"psF{bk}", Xn), w=(Xn,))

            dbg_store("h2", X[:], 1024, "sp", t0)
            mark(6)
            norm_to_T(Xn, X, gffn, "gffn", nT, "nT", T)
            for pg in range(6):
                npair = 4 if pg < 5 else 2
                wn_g, wv_g = next_slab()
                wn_u, wv_u = next_slab(hold=1)
                for pj in range(npair):
                    i = pg * 4 + pj
                    for which, (wn_, wv_) in enumerate(((wn_g, wv_g), (wn_u, wv_u))):
                        chn = i + 22 * which
                        bk = mmbank()
                        mm_fm(wn_, wv_, pj * 128, nT, "nT", 8, bk)
                        hb, ca = hbuf[which], cacc[which]
                        POOL(lambda e, hb=hb, chn=chn: e.tensor_copy(out=hb[:, 0:2], in_=carry[:, chn, :]), r=("carry",), w=(f"hbuf{which}",))
                        ACT(lambda e, hb=hb, bk=bk: e.activation(out=hb[:, 2:T + 2], in_=psF[bk][:, 0:T], func=AF.Identity),
                            r=(f"psF{bk}",), w=(f"hbuf{which}",))
                        ACT(lambda e, ca=ca, bk=bk, chn=chn: e.activation(out=ca[:], in_=psF[bk][:, 0:T], func=AF.Identity,
                                                                          scale=cw[:, chn, 2:3], bias=cb[:, chn:chn + 1]),
                            r=(f"psF{bk}", "cw", "cb"), w=(f"cacc{which}",))
                        DVE(lambda e, ca=ca, hb=hb, chn=chn: e.scalar_tensor_tensor(out=ca[:], in0=hb[:, 1:T + 1], scalar=cw[:, chn, 1:2], in1=ca[:],
                                                                                    op0=ALU.mult, op1=ALU.add),
                            r=(f"hbuf{which}", "cw", f"cacc{which}"), w=(f"cacc{which}",))
                        DVE(lambda e, ca=ca, hb=hb, chn=chn: e.scalar_tensor_tensor(out=ca[:], in0=hb[:, 0:T], scalar=cw[:, chn, 0:1], in1=ca[:],
                                                                                     op0=ALU.mult, op1=ALU.add),
                             r=(f"hbuf{which}", "cw", f"cacc{which}"), w=(f"cacc{which}",))
                        POOL(lambda e, hb=hb, chn=chn: e.tensor_copy(out=carry[:, chn, :], in_=hb[:, T:T + 2]), r=(f"hbuf{which}",), w=("carry",))
                    ACT(lambda e: e.activation(out=gact[:], in_=cacc[0][:], func=AF.Gelu), r=("cacc0",), w=("f32a",))
                    DVE(lambda e, i=i: e.tensor_tensor(out=aT[:, i, :], in0=gact[:], in1=cacc[1][:], op=ALU.mult), r=("f32a", "cacc1"), w=("aT",))
            if nxt is not None:
                norm_p1(Xqn, Xq, T)
            for ch in range(2):
                for kg in range(3):
                    nk = 8 if kg < 2 else 6
                    wn, wv = next_slab()
                    for s_ in range(2):
                        mm_tm(wn, wv, aT, "aT", s_, nk, 4 + s_, kofs=kg * 8, first=(kg == 0), last=(kg == 2))
                for s_ in range(2):
                    DVE(lambda e, s_=s_, ch=ch: e.tensor_tensor(out=X[:, s_, ch * 512:(ch + 1) * 512], in0=psF[4 + s_][:, 0:512],
                                                                in1=X[:, s_, ch * 512:(ch + 1) * 512], op=ALU.add),
                        r=(f"psF{4 + s_}", Xn), w=(Xn,))
            if nxt is not None:
                norm_p2(gmix, "gmix", nT, "nT", T)
            dbg_store("h3", X[:], 1024, "sp", t0)
            mute[0] = False
            DMA("sp", "gf0", f32a[:], AP(gfin_d.tensor, 0, [[0, 128], [1, 512]]), w=("f32a",))
            DMA("sp", "gf1", f32b[:], AP(gfin_d.tensor, 512, [[0, 128], [1, 512]]), w=("f32b",))
            rms_stats(Xn, X, 2, D)
            for s_ in range(2):
                DVE(lambda e, s_=s_: e.tensor_scalar(out=X[:, s_, :], in0=X[:, s_, :], scalar1=stat[:, 8 + s_:9 + s_], scalar2=None, op0=ALU.mult),
                    r=(Xn, "stat_r"), w=(Xn,))
                POOL(lambda e, s_=s_: e.tensor_tensor(out=X[:, s_, 0:512], in0=X[:, s_, 0:512], in1=f32a[:], op=ALU.mult), r=(Xn, "f32a"), w=(Xn,))
                POOL(lambda e, s_=s_: e.tensor_tensor(out=X[:, s_, 512:1024], in0=X[:, s_, 512:1024], in1=f32b[:], op=ALU.mult), r=(Xn, "f32b"), w=(Xn,))
            DMA("sp", f"st{tt % 2}", y_d[t0:t0 + T, :].rearrange("(s p) d -> p s d", p=128), X[:], r=(Xn,), w=("ydram",))

        P.emit("sp", None, [K("ydram"), K("xs0"), K("xs1"), K("dbgdram")], [K("xs0"), K("xs1"), K("ytok")])

        def semfor(key, val):
            kind, nm = key
            if kind == "dma":
                return dsem[nm], val
            ep = (val - 1) // EPOCH
            return esem[nm][ep], val - ep * EPOCH

        block = es.enter_context(nc.Block())

        def replay(eng_name, eng):
            for waits, fn, tok in P.q[eng_name]:
                emb = None
                if fn is not None and waits and EMBED_WAIT and eng_name != "pe":
                    emb = waits[-1]
                    waits = waits[:-1]
                for k, v in waits:
                    s_h, s_v = semfor(k, v)
                    eng.wait_ge(s_h, s_v)
                if fn is None:
                    continue
                ins = fn(eng)
                if emb is not None:
                    s_h, s_v = semfor(emb[0], emb[1])
                    ins._wait_ge(s_h, s_v)
                s_h, _ = semfor(tok[0], tok[1])
                ins.then_inc(s_h, 16 if tok[0][0] == "dma" else 1)

        for e_ in Prog.ENG:
            assert (P.cnt[e_] + EPOCH - 1) // EPOCH <= nep[e_], (e_, P.cnt[e_])

        @block.sync
        def _(e):
            replay("sp", e)

        @block.tensor
        def _(e):
            replay("pe", e)

        @block.scalar
        def _(e):
            replay("act", e)

        @block.vector
        def _(e):
            replay("dve", e)

        @block.gpsimd
        def _(e):
            replay("pool", e)

    return nc, P


def _prep_inputs(inputs):
    f = lambda a: np.ascontiguousarray(np.asarray(a, dtype=np.float32))
    col = lambda g: np.ascontiguousarray(f(g).reshape(8, 128).T)
    rel_bias = f(inputs["rel_bias"])
    kk = np.arange(128)[:, None]
    jj = np.arange(TABL)[None, :]
    d = jj - 128 - kk
    bidx = _rel_bucket_np(d)
    bsk = np.empty((8, 128, TABL), np.float32)
    for h in range(8):
        bsk[h] = np.where(d >= 0, rel_bias[bidx, h], np.float32(NEG))
    conv_w = f(inputs["conv_w"])[0]
    conv_b = f(inputs["conv_b"])[0]
    cw = np.ascontiguousarray(conv_w.reshape(3, 44, 128).transpose(2, 1, 0)).reshape(128, 44 * 3)
    cb = np.ascontiguousarray(conv_b.reshape(44, 128).T)
    shared = {
        "w_in": f(inputs["w_in"])[0], "w_mem_kv": f(inputs["w_mem_kv"])[0],
        "w_br_attn": f(inputs["w_br_attn"])[0], "w_br_ret": f(inputs["w_br_ret"])[0], "w_br_mem": f(inputs["w_br_mem"])[0],
        "w_out": f(inputs["w_out"])[0], "w_up": f(inputs["w_up"])[0], "w_down": f(inputs["w_down"])[0],
        "gmix_col": col(inputs["g_mix"]), "gffn_col": col(inputs["g_ffn"]), "gmem_col": col(inputs["g_mem"]),
        "g_final": f(inputs["g_final"]).reshape(1, D), "ret_gn_gain": f(inputs["ret_gn_gain"]).reshape(1, 512),
        "cw_col": cw, "cb_col": cb, "rb31": np.ascontiguousarray(rel_bias[31:32, :]), "bias_skew": bsk,
        "c_ident": _HC["ident"], "c_mask01T": _HC["mask01T"], "c_sel": _HC["sel"], "c_candmask": _HC["candmask"],
        "c_rt_cos": _HC["rt_cos"], "c_rt_sin": _HC["rt_sin"], "c_scl": _HC["scl"],
    }
    x = f(inputs["x"])
    mem = f(inputs["mem"])
    maps = []
    for b in range(NB):
        m = dict(shared)
        m["x"] = x[b]
        m["mem"] = mem[b]
        maps.append(m)
    return maps


def kernel(**inputs):
    maps = _prep_inputs(inputs)
    nc, _ = build_program()
    res = run_bass_kernel_spmd(nc, maps, core_ids=list(range(NB)))
    return np.stack([np.asarray(r["y"], dtype=np.float32) for r in res.results], axis=0)
```

```python
import contextlib
import math
import numpy as np
import concourse.bass as bass
import concourse.mybir as mybir
from concourse.ap import AP
from concourse.bass_utils import run_bass_kernel_spmd

F32 = mybir.dt.float32
BF16 = mybir.dt.bfloat16
AF = mybir.ActivationFunctionType
ALU = mybir.AluOpType
AX = mybir.AxisListType

NB = 8
S = 4096
D = 1024
T = 256
NT = S // T
DIN = 7168
DFF = 2816
MEM = 256
EPS = 1e-6
NEG = -30000.0
EPOCH = 2000
EMBED_WAIT = True
import os as _os
NOSEEN = bool(int(_os.environ.get("KNOSEEN", "0")))


def _rel_bucket_np(d):
    d = np.maximum(d, 0)
    df = np.maximum(d, 1).astype(np.float32)
    large = 16 + (np.log(df / np.float32(16)) / np.float32(math.log(2048 / 16)) * np.float32(16)).astype(np.int32)
    large = np.minimum(large, 31)
    return np.where(d < 16, d, large)


_bk = _rel_bucket_np(np.arange(0, 4096))
D31 = int(np.min(np.nonzero(_bk == 31)[0]))
assert np.all(_bk[D31:] == 31)
DELTA_FAR = ((D31 + 127 + 127) // 128) * 128
TABL = DELTA_FAR + 255 + 128 + 1


def _host_consts():
    c = {}
    c["ident"] = np.eye(128, dtype=np.float32)
    jj = np.arange(128)
    c["mask01T"] = (jj[:, None] <= jj[None, :]).astype(np.float32)
    sel = np.zeros((128, 16, 128), np.float32)
    for n in range(16):
        sel[n, n, :] = 1.0
    c["sel"] = sel.reshape(128, 16 * 128)
    cm = np.zeros((128, NT, 16), np.float32)
    for tt in range(NT):
        cm[:, tt, tt:] = -1e30
    c["candmask"] = cm.reshape(128, NT * 16)
    pos = np.arange(S, dtype=np.float64)
    half = 64
    inv = (10000.0 ** (-np.arange(half, dtype=np.float32) / np.float32(half))).astype(np.float32)
    ang = (pos.astype(np.float32)[:, None] * inv[None, :]).astype(np.float32).astype(np.float64)
    cos, sin = np.cos(ang), np.sin(ang)
    p = (np.arange(S) % 128).astype(np.float64)
    lg = np.log1p(-np.power(2.0, -5.0 - np.arange(4, dtype=np.float64)))
    up = np.exp(-lg[None, :] * (127.0 - p[:, None]))
    dn = np.exp(lg[None, :] * (127.0 - p[:, None])) / math.sqrt(128.0)
    c["rt_cos"] = cos.astype(np.float32)
    c["rt_sin"] = sin.astype(np.float32)
    c["scl"] = np.concatenate([up[:128], dn[:128]], axis=1).astype(np.float32)
    c["gamC"] = np.exp(lg * 128.0)
    return c


_HC = _host_consts()
GAMC = [float(v) for v in _HC["gamC"]]


import types as _types


def _freeze(fn):
    if fn is None or fn.__closure__ is None:
        return fn
    cells = []
    for c in fn.__closure__:
        try:
            cells.append(_types.CellType(c.cell_contents))
        except ValueError:
            cells.append(c)
    return _types.FunctionType(fn.__code__, fn.__globals__, fn.__name__, fn.__defaults__, tuple(cells))


class Tk:
    __slots__ = ("name", "w", "r", "alias")

    def __init__(self, name):
        self.name = name
        self.w = None
        self.r = {}
        self.alias = []


class Prog:
    ENG = ("pe", "act", "dve", "pool", "sp")

    def __init__(self):
        self.q = {e: [] for e in self.ENG}
        self.cnt = {e: 0 for e in self.ENG}
        self.seen = {e: {} for e in self.ENG}
        self.dcnt = {}

    def emit(self, eng, fn, reads=(), writes=(), dma=None):
        fn = _freeze(fn)
        deps = {}

        def add(tok):
            if tok is None:
                return
            k, v = tok
            if deps.get(k, 0) < v:
                deps[k] = v

        for t in reads:
            add(t.w)
            for a in t.alias:
                add(a.w)
            if t.name.startswith("ps"):
                for k, v in t.r.items():
                    if k != ("eng", eng):
                        add((k, v))
        for t in writes:
            add(t.w)
            for k, v in t.r.items():
                add((k, v))
            for a in t.alias:
                add(a.w)
                for k, v in a.r.items():
                    add((k, v))
        if dma is not None:
            add((("dma", dma), self.dcnt.get(dma, 0)))
        waits = []
        seen = self.seen[eng]
        for k, v in deps.items():
            if v <= 0:
                continue
            if k == ("eng", "pe") and eng == "pe":
                continue
            if seen.get(k, 0) >= v and not NOSEEN:
                continue
            seen[k] = v
            waits.append((k, v))
        if dma is None:
            self.cnt[eng] += 1
            tok = (("eng", eng), self.cnt[eng])
        else:
            self.dcnt[dma] = self.dcnt.get(dma, 0) + 16
            tok = (("dma", dma), self.dcnt[dma])
        self.q[eng].append((waits, fn, tok))
        for t in reads:
            if t.r.get(tok[0], 0) < tok[1]:
                t.r[tok[0]] = tok[1]
        for t in writes:
            t.w = tok
            t.r = {}
        return tok


def _b(ap, dims):
    return AP(ap.tensor, ap.offset, [list(ap.ap[0])] + [list(d) for d in dims])


def build_program(debug=None, ntiles=NT):
    nc = bass.Bass("TRN2", target_bir_lowering=False)
    P = Prog()

    def din(name, shape, dt=F32):
        return nc.dram_tensor(name, list(shape), dt, kind="ExternalInput").ap()

    x_d = din("x", [S, D])
    mem_d = din("mem", [MEM, D])
    w_in_d = din("w_in", [D, DIN])
    w_kv_d = din("w_mem_kv", [D, 1024])
    w_bra_d = din("w_br_attn", [512, D])
    w_brr_d = din("w_br_ret", [512, D])
    w_brm_d = din("w_br_mem", [512, D])
    w_out_d = din("w_out", [D, D])
    w_up_d = din("w_up", [D, 2 * DFF])
    w_dn_d = din("w_down", [DFF, D])
    gmix_d = din("gmix_col", [128, 8])
    gffn_d = din("gffn_col", [128, 8])
    gmem_d = din("gmem_col", [128, 8])
    gfin_d = din("g_final", [1, D])
    gn_d = din("ret_gn_gain", [1, 512])
    cw_d = din("cw_col", [128, 44 * 3])
    cb_d = din("cb_col", [128, 44])
    rb31_d = din("rb31", [1, 8])
    bsk_d = din("bias_skew", [8, 128, TABL])
    ident_d = din("c_ident", [128, 128])
    m01_d = din("c_mask01T", [128, 128])
    sel_d = din("c_sel", [128, 16 * 128])
    cand_d = din("c_candmask", [128, NT * 16])
    rt_d = [din("c_rt_" + k, [S, 64]) for k in ("cos", "sin")]
    scl_d = din("c_scl", [128, 8])
    y_d = nc.dram_tensor("y", [S, D], F32, kind="ExternalOutput").ap()
    dbg_d = nc.dram_tensor("dbg", [S, D], F32, kind="ExternalOutput").ap() if debug else None

    def dscr(name, shape):
        return nc.dram_tensor(name, list(shape), BF16, kind="Internal").ap()

    wb_in = dscr("wb_in", [D, DIN])
    wb_kv = dscr("wb_kv", [D, 1024])
    wb_bra = dscr("wb_bra", [512, D])
    wb_brr = dscr("wb_brr", [512, D])
    wb_brm = dscr("wb_brm", [512, D])
    wb_out = dscr("wb_out", [D, D])
    wb_up = dscr("wb_up", [D, 2 * DFF])
    wb_dn = dscr("wb_dn", [DFF, D])
    bsk_b = dscr("bsk_b", [8, 128, TABL])

    es = contextlib.ExitStack()
    with es:
        def sb(name, shape, dt):
            return es.enter_context(nc.sbuf_tensor(name, list(shape), dt))

        def sem(name):
            return es.enter_context(nc.semaphore(name))

        kpair = [sb(f"kpair{j}", [128, S], BF16) for j in range(4)]
        Va = sb("Va", [128, 32, 8, 65], BF16)
        xsb = [sb(f"xs{i}", [128, 2, D], F32) for i in range(2)]
        nT = sb("nT", [128, 8, T], BF16)
        wbuf = [sb(f"wbuf{i}", [128, 4096], BF16) for i in range(3)]
        qaug = [sb(f"qaug{h}", [128, T], BF16) for h in range(8)]
        maskT = [sb(f"maskT{h}", [128, T], BF16) for h in range(8)]
        maskpad = sb("maskpad", [128, 8, 128], BF16)
        sel = sb("sel", [128, 16, 128], BF16)
        kmeanT = [sb(f"kmeanT{j}", [128, 16], BF16) for j in range(4)]
        qhT = sb("qhT", [128, 4, T], BF16)
        khT = sb("khT", [128, 4, T], BF16)
        khtok = sb("khtok", [128, 2, 512], BF16)
        vrtok = sb("vrtok", [128, 2, 512], BF16)
        gs = sb("gs", [128, 2, 512], BF16)
        qmT = sb("qmT", [128, 4, T], BF16)
        tz = sb("tz", [128, 3, 8, T], BF16)
        yT = [sb(f"yT{i}", [128, 4, T], BF16) for i in range(3)]
        ytok1 = sb("ytok", [128, 2, 512], BF16)
        ytok = [ytok1, ytok1, ytok1]
        mergedT = sb("mergedT", [128, 8, T], BF16)
        btab = [sb(f"btab{i}", [128, TABL], BF16) for i in range(2)]
        rtab = [sb(f"rtab{i}", [128, 2, 64], F32) for i in range(2)]
        scl = sb("scl", [128, 8], F32)
        sbias = [sb(f"sbias{i}", [128, T], F32) for i in range(2)]
        pT = [sb(f"pT{i}", [128, T], BF16) for i in range(4)]
        KmT = sb("KmT", [128, 4, MEM], BF16)
        Vm = sb("Vm", [128, 2, 4, 129], BF16)
        aT = sb("aT", [128, 22, T], BF16)
        hbuf = [sb(f"hbuf{i}", [128, T + 2], F32) for i in range(2)]
        cacc = [sb(f"cacc{i}", [128, T], F32) for i in range(2)]
        carry = sb("carry", [128, 44, 2], F32)
        Sst = sb("Sst", [128, 4, 128], F32)
        Sbf = sb("Sbf", [128, 4, 128], BF16)
        AT = sb("AT", [128, 4, 128], BF16)
        f32a = sb("f32a", [128, 512], F32)
        f32b = sb("f32b", [128, 512], F32)
        ysq = f32a
        xrot = f32a
        yc = f32b
        tg = f32a
        ug = f32b
        xn = sb("xn", [128, 2, D], BF16)
        stat = sb("stat", [128, 32], F32)
        gsb = sb("gsb", [128, 8, 16], F32)
        top8 = sb("top8", [128, 8, 8], F32)
        rden = sb("rden", [128, 2], F32)
        ident = sb("ident", [128, 128], BF16)
        m01 = sb("m01", [128, 128], F32)
        cand = sb("cand", [128, 16], F32)
        gmix = sb("gmixc", [128, 8], F32)
        gffn = sb("gffnc", [128, 8], F32)
        gmem = sb("gmemc", [128, 8], F32)
        gnh = sb("gnh", [128, 512], F32)
        cw = sb("cw", [128, 44, 3], F32)
        cb = sb("cb", [128, 44], F32)
        rb31 = sb("rb31_sb", [128, 8], F32)
        memnT = aT

        rtmp = [aT[:, 0:2, :].rearrange("p a b -> p (a b)").bitcast(F32), aT[:, 2:4, :].rearrange("p a b -> p (a b)").bitcast(F32)]
        rottok = aT[:, 4:8, :].rearrange("p (s a) b -> p s (a b)", s=2)
        mtmp = f32a[:, 0:T]
        mtmp2 = f32b[:, 0:T]
        gact = f32a[:, T:2 * T]

        sb2 = [aT[:, 8:12, :].rearrange("p a b -> p (a b)").bitcast(F32), aT[:, 12:16, :].rearrange("p a b -> p (a b)").bitcast(F32)]
        pT2 = [sbias[0][:].bitcast(BF16), sbias[1][:].bitcast(BF16),
               aT[:, 16:18, :].rearrange("p a b -> p (a b)"), aT[:, 18:20, :].rearrange("p a b -> p (a b)")]
        pT2n = ["sbias0", "sbias1", "pT2_2", "pT2_3"]

        psF = [es.enter_context(nc.psum_tensor(f"psF{i}", [128, 512], F32)) for i in range(6)]
        psB = [es.enter_context(nc.psum_tensor(f"psB{i}", [128, 1024], BF16)) for i in range(2)]

        nep = {"pe": 18, "act": 8, "dve": 10, "pool": 6, "sp": 1}
        esem = {e: [sem(f"s_{e}{i}") for i in range(nep[e])] for e in Prog.ENG}
        dsem = {}

        def dma_sem(name):
            if name not in dsem:
                dsem[name] = sem("d_" + name)
            return name

        tk = {}

        def K(obj_name):
            if obj_name not in tk:
                tk[obj_name] = Tk(obj_name)
            return tk[obj_name]

        for _n in ("rtmp0", "rtmp1", "rottok", "sb2_0", "sb2_1", "pT2_2", "pT2_3"):
            K(_n).alias.append(K("aT"))
            K("aT").alias.append(K(_n))

        mute = [False]
        kcut = int(_os.environ.get("KCUT", "99"))

        def PE(fn, r=(), w=()):
            if mute[0]:
                return
            P.emit("pe", fn, [K(a) for a in r], [K(a) for a in w])

        def ACT(fn, r=(), w=()):
            if mute[0]:
                return
            P.emit("act", fn, [K(a) for a in r], [K(a) for a in w])

        def DVE(fn, r=(), w=()):
            if mute[0]:
                return
            P.emit("dve", fn, [K(a) for a in r], [K(a) for a in w])

        def POOL(fn, r=(), w=()):
            if mute[0]:
                return
            P.emit("pool", fn, [K(a) for a in r], [K(a) for a in w])

        def DMA(eng, semname, out, in_, r=(), w=()):
            if mute[0]:
                return
            dma_sem(semname)
            P.emit(eng, lambda e, out=out, in_=in_: e.dma_start(out=out, in_=in_),
                   [K(a) for a in r], [K(a) for a in w], dma=semname)

        mmrot = [0]

        def mmbank():
            i = mmrot[0] % 4
            mmrot[0] += 1
            return i

        trrot = [0]

        def trbank():
            i = trrot[0] % 2
            trrot[0] += 1
            return i

        wsrc = {"kv": w_kv_d, "in": w_in_d, "bra": w_bra_d, "brr": w_brr_d, "brm": w_brm_d, "out": w_out_d, "up": w_up_d, "dn": w_dn_d}
        conv_i = [0]
        for h in range(8):
            DMA("pool", f"cv{conv_i[0] % 4}", bsk_b[h], bsk_d[h], r=(), w=(f"cv_bsk{h}",))
            conv_i[0] += 1

        sdesc = []
        for si in range(2):
            sdesc.append(("kv", 0, 8, si * 512, 512))
        NSETUP = len(sdesc)
        for si in range(14):
            sdesc.append(("in", 0, 8, si * 512, 512))
        sdesc += [("bra", 0, 4, 0, 1024), ("brr", 0, 4, 0, 1024), ("brm", 0, 4, 0, 1024)]
        for ch in range(2):
            sdesc.append(("out", 0, 8, ch * 512, 512))
        for pg in range(6):
            ncol = 512 if pg < 5 else 256
            sdesc.append(("up", 0, 8, pg * 512, ncol))
            sdesc.append(("up", 0, 8, DFF + pg * 512, ncol))
        for ch in range(2):
            for kg in range(3):
                sdesc.append(("dn", kg * 8, 8 if kg < 2 else 6, ch * 512, 512))
        NDIST = len(sdesc)
        NPER = NDIST - NSETUP
        wsc = nc.dram_tensor("wsc", [NDIST, 128, 4096], BF16, kind="Internal").ap()
        conv_names = {}

        def ensure_conv(j):
            if j in conv_names:
                return conv_names[j]
            key, r0, nk, c0, ncol = sdesc[j]
            sv = wsrc[key].rearrange("(kc p) c -> p kc c", p=128)
            names = []
            for k0 in range(0, nk, 2):
                kk = min(2, nk - k0)
                name = f"cvs{j}_{k0}"
                DMA("pool", f"cv{conv_i[0] % 6}", wsc[j][:, k0 * ncol:(k0 + kk) * ncol].rearrange("p (k c) -> p k c", k=kk),
                    sv[:, r0 + k0:r0 + k0 + kk, c0:c0 + ncol], r=(), w=(name,))
                conv_i[0] += 1
                names.append(name)
            conv_names[j] = tuple(names)
            return conv_names[j]

        NSLAB = NSETUP + NPER * NT

        def distinct(i):
            return i if i < NSETUP else NSETUP + (i - NSETUP) % NPER

        ws = {"issued": 0, "used": 0}

        def issue_loads(upto):
            for i2 in range(ws["issued"], min(upto + 4, NDIST)):
                ensure_conv(i2)
            while ws["issued"] < min(upto, NSLAB):
                i = ws["issued"]
                j = distinct(i)
                key, r0, nk, c0, ncol = sdesc[j]
                slot = i % 3
                DMA("sp", f"wb{slot}", wbuf[slot][:, 0:nk * ncol], wsc[j][:, 0:nk * ncol], r=ensure_conv(j), w=(f"wbuf{slot}",))
                ws["issued"] += 1

        def next_slab(hold=0):
            i = ws["used"]
            issue_loads(i + 3 - hold)
            ws["used"] += 1
            key, r0, nk, c0, ncol = sdesc[distinct(i)]
            slot = i % 3
            return f"wbuf{slot}", wbuf[slot][:, 0:nk * ncol].rearrange("p (k c) -> p k c", k=nk)

        def ld(dst_ap, src_ap, name, eng="sp"):
            DMA(eng, "c_" + name, dst_ap, src_ap, w=(name,))

        ld(ident[:], ident_d[:, :], "ident", eng="pool")
        ld(m01[:], m01_d[:, :], "m01")
        ld(sel[:].rearrange("p a b -> p (a b)"), sel_d[:, :], "sel", eng="pool")
        ld(scl[:], scl_d[:, :], "scl")
        ld(gmix[:], gmix_d[:, :], "gmix")
        ld(gffn[:], gffn_d[:, :], "gffn")
        ld(gmem[:], gmem_d[:, :], "gmem")
        ld(cw[:].rearrange("p a b -> p (a b)"), cw_d[:, :], "cw")
        ld(cb[:], cb_d[:, :], "cb")
        ld(gnh[:], AP(gn_d.tensor, 0, [[0, 128], [1, 512]]), "gnh")
        ld(rb31[:], AP(rb31_d.tensor, 0, [[0, 128], [1, 8]]), "rb31")
        ld(xsb[0][:], mem_d.rearrange("(s p) d -> p s d", p=128), "xs0")

        POOL(lambda e: e.tensor_scalar(out=gnh[:], in0=gnh[:], scalar1=0.5, scalar2=None, op0=ALU.mult), r=("gnh",), w=("gnh",))
        POOL(lambda e: e.memset(carry[:].rearrange("p a b -> p (a b)"), 0.0), w=("carry",))
        POOL(lambda e: e.memset(Sst[:].rearrange("p a b -> p (a b)"), 0.0), w=("Sst",))
        POOL(lambda e: e.memset(Sbf[:].rearrange("p a b -> p (a b)"), 0.0), w=("Sbf",))
        POOL(lambda e: e.memset(Va[:].rearrange("p a b c -> p (a b c)"), 1.0), w=("Va",))
        POOL(lambda e: e.memset(Vm[:].rearrange("p a b c -> p (a b c)"), 1.0), w=("Vm",))
        for h in range(8):
            POOL(lambda e, h=h: e.memset(qaug[h][:], 0.0), w=(f"qaug{h}",))
        for j in range(4):
            POOL(lambda e, j=j: e.memset(kmeanT[j][:], 0.0), w=(f"kmeanT{j}",))
        POOL(lambda e: e.memset(maskpad[:].rearrange("p a b -> p (a b)"), 0.0), w=("maskpad",))

        def rms_stats(src_name, src, nsub, width):
            for s_ in range(nsub):
                ACT(lambda e, s_=s_: e.activation(out=xn[:, s_, 0:width], in_=src[:, s_, :], func=AF.Square,
                                                  accum_out=stat[:, s_:s_ + 1]),
                    r=(src_name,), w=(f"xn{s_}", f"stat{s_}"))
            names = tuple(f"stat{s_}" for s_ in range(nsub))
            DVE(lambda e: e.tensor_scalar(out=stat[:, 4:4 + nsub], in0=stat[:, 0:nsub], scalar1=1.0 / width,
                                          scalar2=EPS, op0=ALU.mult, op1=ALU.add), r=names, w=("stat_v",))
            ACT(lambda e: e.activation(out=stat[:, 4:4 + nsub], in_=stat[:, 4:4 + nsub], func=AF.Sqrt),
                r=("stat_v",), w=("stat_v",))
            DVE(lambda e: e.reciprocal(out=stat[:, 8:8 + nsub], in_=stat[:, 4:4 + nsub]), r=("stat_v",), w=("stat_r",))

        def norm_p1(src_name, src, ntok):
            nsub = ntok // 128
            rms_stats(src_name, src, nsub, D)
            for s_ in range(nsub):
                DVE(lambda e, s_=s_: e.tensor_scalar(out=xn[:, s_, :], in0=src[:, s_, :], scalar1=stat[:, 8 + s_:9 + s_],
                                                     scalar2=None, op0=ALU.mult), r=(src_name, "stat_r"), w=(f"xn{s_}",))

        def norm_to_T(src_name, src, gcol, gname, dst, dst_name, ntok):
            norm_p1(src_name, src, ntok)
            norm_p2(gcol, gname, dst, dst_name, ntok)

        def norm_p2(gcol, gname, dst, dst_name, ntok):
            nsub = ntok // 128
            for kc in range(8):
                b = trbank()
                for s_ in range(nsub):
                    PE(lambda e, kc=kc, s_=s_, b=b: e.transpose(out=psB[b][:, s_ * 128:(s_ + 1) * 128],
                                                                in_=xn[:, s_, kc * 128:(kc + 1) * 128], identity=ident[:]),
                       r=(f"xn{s_}", "ident"), w=(f"psB{b}",))
                ACT(lambda e, kc=kc, b=b: e.activation(out=dst[:, kc, 0:ntok], in_=psB[b][:, 0:ntok], func=AF.Identity,
                                                        scale=gcol[:, kc:kc + 1]), r=(f"psB{b}", gname), w=(dst_name,))

        def mm_fm(wname, wv, c0, rhs_t, rhs_name, nk, bank, ntok=T):
            for kc in range(nk):
                PE(lambda e, kc=kc: e.matmul(psF[bank][:, 0:ntok], lhsT=wv[:, kc, c0:c0 + 128], rhs=rhs_t[:, kc, 0:ntok],
                                             start=(kc == 0), stop=(kc == nk - 1)),
                   r=(wname, rhs_name), w=(f"psF{bank}",))

        def mm_tm(wname, wv, lhs_t, lhs_name, s_, nk, bank, ncol=512, kofs=0, first=True, last=True):
            for kc in range(nk):
                PE(lambda e, kc=kc: e.matmul(psF[bank][:, 0:ncol], lhsT=lhs_t[:, kofs + kc, s_ * 128:(s_ + 1) * 128],
                                             rhs=wv[:, kc, 0:ncol], start=(first and kc == 0), stop=(last and kc == nk - 1)),
                   r=(wname, lhs_name), w=(f"psF{bank}",))

        def transpose_tok(src, src_name, dst, dst_name, nchunk):
            for j in range(nchunk):
                b = trbank()
                for s_ in range(2):
                    PE(lambda e, j=j, s_=s_, b=b: e.transpose(out=psB[b][:, s_ * 128:(s_ + 1) * 128],
                                                              in_=src[:, s_, j * 128:(j + 1) * 128], identity=ident[:]),
                       r=(src_name, "ident"), w=(f"psB{b}",))
                DVE(lambda e, j=j, b=b: e.tensor_copy(out=dst[:, j, :], in_=psB[b][:, 0:T]), r=(f"psB{b}",), w=(dst_name,))

        norm_to_T("xs0", xsb[0], gmem, "gmem", memnT, "aT", MEM)
        wn, wv = next_slab()
        for h in range(4):
            bk = mmbank()
            mm_fm(wn, wv, h * 128, memnT, "aT", 8, bk, ntok=MEM)
            ACT(lambda e, h=h, bk=bk: e.activation(out=KmT[:, h, :], in_=psF[bk][:, 0:MEM], func=AF.Identity),
                r=(f"psF{bk}",), w=("KmT",))
        wn, wv = next_slab()
        for mc in range(2):
            bk = mmbank()
            mm_tm(wn, wv, memnT, "aT", mc, 8, bk)
            ACT(lambda e, mc=mc, bk=bk: e.activation(out=Vm[:, mc, :, 0:128],
                                                      in_=psF[bk][:, 0:512].rearrange("p (h e) -> p h e", h=4),
                                                      func=AF.Identity), r=(f"psF{bk}",), w=("Vm",))

        LA = 3

        def dbg_store(tag, src_ap, ncol, eng, t0):
            if debug == tag:
                DMA(eng, "dbg", dbg_d[t0:t0 + T, 0:ncol].rearrange("(s p) d -> p s d", p=128), src_ap, r=("ytok", "xs0", "xs1"), w=("dbgdram",))

        _tl = [int(v) for v in _os.environ["KTILES"].split(",")] if _os.environ.get("KTILES") else list(range(ntiles))
        def barrier():
            allt = list(tk.values())
            P.emit("act", lambda e: e.activation(out=stat[:, 30:31], in_=stat[:, 30:31], func=AF.Identity), [], allt)
            P.emit("dve", lambda e: e.memset(stat[:, 31:32], 0.0), [], allt)
            P.emit("pool", lambda e: e.memset(stat[:, 29:30], 0.0), [], allt)
            P.emit("sp", None, [], allt)

        for _ti, tt in enumerate(_tl):
            t0 = tt * T
            if _ti > 0 and _os.environ.get("KBAR"):
                barrier()

            def mark(k, _ti=_ti):
                if _ti > 0 and k >= kcut:
                    mute[0] = True
            X = xsb[tt % 2]
            Xn = f"xs{tt % 2}"
            nxt = _tl[_ti + 1] if _ti + 1 < len(_tl) else None
            if _ti == 0:
                DMA("sp", f"xld{tt % 2}", X[:], x_d[t0:t0 + T, :].rearrange("(s p) d -> p s d", p=128), w=(Xn,))
                norm_to_T(Xn, X, gmix, "gmix", nT, "nT", T)
            if nxt is not None:
                Xq, Xqn = xsb[nxt % 2], f"xs{nxt % 2}"
            if tt > 3:
                DMA("sp", "cand", cand[:], cand_d[:, tt * 16:(tt + 1) * 16], w=("cand",))
            for i in range(2):
                DMA("sp", f"rt{i}", rtab[i][:], rt_d[i][t0:t0 + T, :].rearrange("(s p) c -> p s c", p=128), w=(f"rtab{i}",))
            mark(0)

            mark(1)
            wn, wv = next_slab()
            for j in range(4):
                bk = mmbank()
                mm_fm(wn, wv, j * 128, nT, "nT", 8, bk)
                ACT(lambda e, j=j, bk=bk: e.activation(out=qaug[2 * j][0:64, :], in_=psF[bk][0:64, 0:T], func=AF.Identity,
                                                        scale=0.125), r=(f"psF{bk}",), w=(f"qaug{2 * j}",))
                DVE(lambda e, j=j, bk=bk: e.tensor_scalar(out=qaug[2 * j + 1][64:128, :], in0=psF[bk][64:128, 0:T], scalar1=0.125,
                                                           scalar2=None, op0=ALU.mult), r=(f"psF{bk}",), w=(f"qaug{2 * j + 1}",))
            wn, wv = next_slab()
            for j in range(4):
                bk = mmbank()
                mm_fm(wn, wv, j * 128, nT, "nT", 8, bk)
                ACT(lambda e, j=j, bk=bk: e.activation(out=kpair[j][:, t0:t0 + T], in_=psF[bk][:, 0:T], func=AF.Identity),
                    r=(f"psF{bk}",), w=(f"kpair{j}",))
                DVE(lambda e, bk=bk: e.tensor_reduce(out=stat[:, 12:13], in_=psF[bk][:, 0:T], axis=AX.X, op=ALU.add),
                    r=(f"psF{bk}",), w=("stat_k",))
                DVE(lambda e, j=j: e.tensor_scalar(out=kmeanT[j][:, tt:tt + 1], in0=stat[:, 12:13], scalar1=1.0 / T, scalar2=None,
                                                    op0=ALU.mult), r=("stat_k",), w=(f"kmeanT{j}",))
            wn, wv = next_slab()
            for s_ in range(2):
                bk = mmbank()
                mm_tm(wn, wv, nT, "nT", s_, 8, bk)
                ACT(lambda e, s_=s_, bk=bk: e.activation(out=Va[:, 2 * tt + s_, :, 0:64],
                                                          in_=psF[bk][:, 0:512].rearrange("p (h e) -> p h e", h=8),
                                                          func=AF.Identity), r=(f"psF{bk}",), w=("Va",))
            def gate_p1(s_):
                bk = mmbank()
                for h in range(8):
                    PE(lambda e, h=h: e.matmul(psF[bk][:, h * 16:(h + 1) * 16], lhsT=qaug[h][:, s_ * 128:(s_ + 1) * 128],
                                               rhs=kmeanT[h // 2][:, 0:16], start=True, stop=True),
                       r=(f"qaug{h}", f"kmeanT{h // 2}"), w=(f"psF{bk}",))
                cmv = _b(cand[:, :], [[0, 8], [1, 16]])
                DVE(lambda e: e.tensor_tensor(out=gsb[:], in0=psF[bk][:, 0:128].rearrange("p (h n) -> p h n", h=8),
                                              in1=cmv, op=ALU.add), r=(f"psF{bk}", "cand"), w=("gsb",))
                for h in range(8):
                    DVE(lambda e, h=h: e.max(out=top8[:, h, :], in_=gsb[:, h, :]), r=("gsb",), w=("top8",))
                for h in range(8):
                    DVE(lambda e, h=h: e.tensor_scalar(out=maskpad[:, h, 0:16], in0=gsb[:, h, :], scalar1=top8[:, h, 2:3],
                                                       scalar2=None, op0=ALU.is_ge), r=("gsb", "top8"), w=("maskpad",))
                DVE(lambda e: e.tensor_scalar(out=maskpad[:, :, 0:16], in0=maskpad[:, :, 0:16], scalar1=-1.0,
                                              scalar2=-NEG, op0=ALU.add, op1=ALU.mult), r=("maskpad",), w=("maskpad",))

            def gate_p2(s_):
                for h in range(8):
                    b = trbank()
                    PE(lambda e, h=h: e.transpose(out=psB[b][:, 0:128], in_=maskpad[:, h, :], identity=ident[:]),
                       r=("maskpad", "ident"), w=(f"psB{b}",))
                    ACT(lambda e, h=h: e.activation(out=maskT[h][:, s_ * 128:(s_ + 1) * 128], in_=psB[b][:, 0:128], func=AF.Identity),
                        r=(f"psB{b}",), w=(f"maskT{h}",))

            def ret_a(s_):
                bk = mmbank()
                for h in range(4):
                    PE(lambda e, h=h, s_=s_, bk=bk: e.matmul(psF[bk][:, h * 128:(h + 1) * 128], lhsT=khT[:, h, s_ * 128:(s_ + 1) * 128],
                                                              rhs=qhT[:, h, s_ * 128:(s_ + 1) * 128], start=True, stop=True),
                       r=("khT", "qhT"), w=(f"psF{bk}",))
                DVE(lambda e, bk=bk: e.tensor_tensor(out=AT[:], in0=psF[bk][:, 0:512].rearrange("p (h c) -> p h c", h=4),
                                                     in1=_b(m01[:], [[0, 4], [1, 128]]), op=ALU.mult), r=(f"psF{bk}", "m01"), w=("AT",))
                return bk

            def ret_b(s_):
                for h in range(4):
                    PE(lambda e, h=h, s_=s_: e.matmul(psF[4][:, h * 128:(h + 1) * 128], lhsT=AT[:, h, :], rhs=vrtok[:, s_, h * 128:(h + 1) * 128],
                                                       start=True, stop=False), r=("AT", "vrtok"), w=("psF4",))
                    PE(lambda e, h=h, s_=s_: e.matmul(psF[4][:, h * 128:(h + 1) * 128], lhsT=qhT[:, h, s_ * 128:(s_ + 1) * 128], rhs=Sbf[:, h, :],
                                                       start=False, stop=True), r=("qhT", "Sbf"), w=("psF4",))
                for h in range(4):
                    PE(lambda e, h=h, s_=s_: e.matmul(psF[5][:, h * 128:(h + 1) * 128], lhsT=khtok[:, s_, h * 128:(h + 1) * 128],
                                                       rhs=vrtok[:, s_, h * 128:(h + 1) * 128], start=True, stop=True),
                       r=("khtok", "vrtok"), w=("psF5",))
                for h in range(4):
                    DVE(lambda e, h=h: e.scalar_tensor_tensor(out=Sst[:, h, :], in0=Sst[:, h, :], scalar=GAMC[h], in1=psF[5][:, h * 128:(h + 1) * 128],
                                                              op0=ALU.mult, op1=ALU.add), r=("Sst", "psF5"), w=("Sst",))
                for h in range(4):
                    POOL(lambda e, h=h: e.tensor_scalar(out=Sbf[:, h, :], in0=Sst[:, h, :], scalar1=GAMC[h], scalar2=None, op0=ALU.mult),
                         r=("Sst",), w=("Sbf",))
                pyv = psF[4][:, 0:512].rearrange("p (h e) -> p h e", h=4)
                DVE(lambda e, pyv=pyv: e.tensor_reduce(out=stat[:, 16:20], in_=pyv, axis=AX.X, op=ALU.add), r=("psF4",), w=("gn_s",))
                ACT(lambda e: e.activation(out=ysq[:], in_=psF[4][:, 0:512], func=AF.Square), r=("psF4",), w=("f32a",))
                DVE(lambda e: e.tensor_reduce(out=stat[:, 20:24], in_=ysq[:].rearrange("p (h e) -> p h e", h=4), axis=AX.X, op=ALU.add),
                    r=("f32a",), w=("gn_q",))
                DVE(lambda e: e.tensor_scalar(out=stat[:, 16:20], in0=stat[:, 16:20], scalar1=1.0 / 128, scalar2=None, op0=ALU.mult),
                    r=("gn_s",), w=("gn_s",))
                DVE(lambda e: e.tensor_tensor(out=stat[:, 24:28], in0=stat[:, 16:20], in1=stat[:, 16:20], op=ALU.mult), r=("gn_s",), w=("gn_m2",))
                DVE(lambda e: e.scalar_tensor_tensor(out=stat[:, 20:24], in0=stat[:, 20:24], scalar=1.0 / 128, in1=stat[:, 24:28],
                                                     op0=ALU.mult, op1=ALU.subtract), r=("gn_q", "gn_m2"), w=("gn_q",))
                DVE(lambda e: e.tensor_scalar(out=stat[:, 20:24], in0=stat[:, 20:24], scalar1=EPS, scalar2=None, op0=ALU.add),
                    r=("gn_q",), w=("gn_q",))
                ACT(lambda e: e.activation(out=stat[:, 20:24], in_=stat[:, 20:24], func=AF.Sqrt), r=("gn_q",), w=("gn_q",))
                DVE(lambda e: e.reciprocal(out=stat[:, 24:28], in_=stat[:, 20:24]), r=("gn_q", "gn_m2"), w=("gn_m2",))
                ycv = yc[:].rearrange("p (h e) -> p h e", h=4)
                DVE(lambda e, pyv=pyv, ycv=ycv: e.tensor_tensor(out=ycv, in0=pyv, in1=_b(stat[:, 16:20], [[1, 4], [0, 128]]), op=ALU.subtract),
                    r=("psF4", "gn_s"), w=("f32b",))
                POOL(lambda e, ycv=ycv: e.tensor_tensor(out=ycv, in0=ycv, in1=_b(stat[:, 24:28], [[1, 4], [0, 128]]), op=ALU.mult),
                     r=("f32b", "gn_m2"), w=("f32b",))
                POOL(lambda e, s_=s_: e.tensor_tensor(out=ytok[1][:, s_, :], in0=yc[:], in1=gs[:, s_, :], op=ALU.mult),
                     r=("f32b", "gs"), w=("ytok",))


            def mem_attn():
                mitems = [(h, mc) for h in range(4) for mc in range(2)]

                def mem_s1(h, mc, idx):
                    bk = mmbank()
                    PE(lambda e: e.matmul(psF[bk][:, 0:T], lhsT=KmT[:, h, mc * 128:(mc + 1) * 128], rhs=qmT[:, h, :], start=True, stop=True),
                       r=("KmT", "qmT"), w=(f"psF{bk}",))
                    pi = idx % 4
                    ACT(lambda e: e.activation(out=pT[pi][:], in_=psF[bk][:, 0:T], func=AF.Exp, scale=128.0 ** -0.5),
                        r=(f"psF{bk}",), w=(f"pT{pi}",))

                def mem_s2(h, mc, idx):
                    acc = 4 + (h % 2)
                    pi = idx % 4
                    for s_ in range(2):
                        PE(lambda e, s_=s_: e.matmul(psF[acc][:, s_ * 129:(s_ + 1) * 129], lhsT=pT[pi][:, s_ * 128:(s_ + 1) * 128], rhs=Vm[:, mc, h, :],
                                                     start=(mc == 0 and s_ == 0), stop=(mc == 1), skip_group_check=True),
                           r=(f"pT{pi}", "Vm"), w=(f"psF{acc}",))
                    if mc == 1:
                        pov = psF[acc][:, 0:258].rearrange("p (s e) -> p s e", s=2)
                        DVE(lambda e: e.reciprocal(out=rden[:], in_=pov[:, :, 128]), r=(f"psF{acc}",), w=("rden",))
                        DVE(lambda e: e.tensor_tensor(out=ytok[2][:, :, h * 128:(h + 1) * 128], in0=pov[:, :, 0:128],
                                                      in1=_b(rden[:], [[1, 2], [0, 128]]), op=ALU.mult),
                            r=(f"psF{acc}", "rden"), w=("ytok",))

                for i in range(len(mitems) + LA):
                    if i < len(mitems):
                        mem_s1(mitems[i][0], mitems[i][1], i)
                    if i >= LA:
                        mem_s2(mitems[i - LA][0], mitems[i - LA][1], i - LA)


            def rot_transposes(which):
                src_t = rottok if which == 0 else khtok
                src_name = "rottok" if which == 0 else "khtok"
                dstT = qhT if which == 0 else khT
                dstT_name = "qhT" if which == 0 else "khT"
                for h in range(4):
                    b = trbank()
                    for s_ in range(2):
                        PE(lambda e, s_=s_: e.transpose(out=psB[b][:, s_ * 128:(s_ + 1) * 128], in_=src_t[:, s_, h * 128:(h + 1) * 128],
                                                        identity=ident[:]), r=(src_name, "ident"), w=(f"psB{b}",))
                    ACT(lambda e: e.activation(out=dstT[:, h, :], in_=psB[b][:, 0:T], func=AF.Identity), r=(f"psB{b}",), w=(dstT_name,))

            if tt > 3:
                gate_p1(0)
            for which in range(2):
                wn, wv = next_slab()
                for s_ in range(2):
                    bk = mmbank()
                    mm_tm(wn, wv, nT, "nT", s_, 8, bk)
                    DVE(lambda e, bk=bk, which=which: e.tensor_tensor(out=xrot[:].rearrange("p (h d) -> p h d", h=4),
                                                                      in0=psF[bk][:, 0:512].rearrange("p (h d) -> p h d", h=4),
                                                                      in1=_b(scl[:, which * 4:which * 4 + 4], [[1, 4], [0, 128]]), op=ALU.mult),
                        r=(f"psF{bk}", "scl"), w=("f32a",))
                    xv = xrot[:].rearrange("p (h t i) -> p h t i", h=4, t=2)
                    Ct = _b(rtab[0][:, s_, :], [[0, 4], [1, 64]])
                    St = _b(rtab[1][:, s_, :], [[0, 4], [1, 64]])
                    r4 = [rtmp[i][:].rearrange("p (h i) -> p h i", h=4) for i in range(2)]
                    dst_tok = khtok[:, s_, :] if which == 1 else rottok[:, s_, :]
                    dst_name = "khtok" if which == 1 else "rottok"
                    ov = dst_tok.rearrange("p (h t i) -> p h t i", h=4, t=2)
                    DVE(lambda e, xv=xv, Ct=Ct, r4=r4: e.tensor_tensor(out=r4[0], in0=xv[:, :, 0, :], in1=Ct, op=ALU.mult),
                        r=("f32a", "rtab0"), w=("rtmp0",))
                    POOL(lambda e, xv=xv, St=St, r4=r4: e.tensor_tensor(out=r4[1], in0=xv[:, :, 1, :], in1=St, op=ALU.mult),
                         r=("f32a", "rtab1"), w=("rtmp1",))
                    DVE(lambda e, ov=ov, r4=r4: e.tensor_tensor(out=ov[:, :, 0, :], in0=r4[0], in1=r4[1], op=ALU.subtract),
                        r=("rtmp0", "rtmp1"), w=(dst_name,))
                    POOL(lambda e, xv=xv, St=St, r4=r4: e.tensor_tensor(out=r4[0], in0=xv[:, :, 0, :], in1=St, op=ALU.mult),
                         r=("f32a", "rtab1"), w=("rtmp0",))
                    DVE(lambda e, xv=xv, Ct=Ct, r4=r4: e.tensor_tensor(out=r4[1], in0=xv[:, :, 1, :], in1=Ct, op=ALU.mult),
                        r=("f32a", "rtab0"), w=("rtmp1",))
                    POOL(lambda e, ov=ov, r4=r4: e.tensor_tensor(out=ov[:, :, 1, :], in0=r4[0], in1=r4[1], op=ALU.add),
                         r=("rtmp0", "rtmp1"), w=(dst_name,))
                if tt > 3:
                    if which == 0:
                        gate_p2(0)
                        gate_p1(1)
                    else:
                        gate_p2(1)
                if which == 1:
                    rot_transposes(0)
            wn, wv = next_slab()
            for s_ in range(2):
                bk = mmbank()
                mm_tm(wn, wv, nT, "nT", s_, 8, bk)
                ACT(lambda e, s_=s_, bk=bk: e.activation(out=vrtok[:, s_, :], in_=psF[bk][:, 0:512], func=AF.Identity),
                    r=(f"psF{bk}",), w=("vrtok",))
            rot_transposes(1)
            wn, wv = next_slab()
            for s_ in range(2):
                bk = mmbank()
                mm_tm(wn, wv, nT, "nT", s_, 8, bk)
                ACT(lambda e, bk=bk: e.activation(out=tg[:], in_=psF[bk][:, 0:512], func=AF.Tanh, scale=0.5),
                    r=(f"psF{bk}",), w=("f32a",))
                DVE(lambda e, bk=bk: e.scalar_tensor_tensor(out=ug[:], in0=tg[:], scalar=1.0, in1=psF[bk][:, 0:512],
                                                            op0=ALU.add, op1=ALU.mult), r=("f32a", f"psF{bk}"), w=("f32b",))
                POOL(lambda e, s_=s_: e.tensor_tensor(out=gs[:, s_, :], in0=ug[:], in1=gnh[:], op=ALU.mult),
                     r=("f32b", "gnh"), w=("gs",))
            wn, wv = next_slab()
            for h in range(4):
                bk = mmbank()
                mm_fm(wn, wv, h * 128, nT, "nT", 8, bk)
                ACT(lambda e, h=h, bk=bk: e.activation(out=qmT[:, h, :], in_=psF[bk][:, 0:T], func=AF.Identity),
                    r=(f"psF{bk}",), w=("qmT",))
            ret_a(0)
            for zi in range(3):
                for half in range(2):
                    wn, wv = next_slab()
                    for jp in range(2):
                        bk = mmbank()
                        for hf in range(2):
                            c0 = (2 * jp + hf) * 128
                            for kc in range(8):
                                PE(lambda e, bk=bk, hf=hf, c0=c0, kc=kc, wv=wv: e.matmul(psF[bk][:, hf * T:(hf + 1) * T], lhsT=wv[:, kc, c0:c0 + 128],
                                                                                      rhs=nT[:, kc, :], start=(kc == 0), stop=(kc == 7), skip_group_check=True),
                                   r=(wn, "nT"), w=(f"psF{bk}",))
                        fc0 = half * 4 + 2 * jp
                        ACT(lambda e, zi=zi, fc0=fc0, bk=bk: e.activation(out=tz[:, zi, fc0:fc0 + 2, :].rearrange("p a b -> p (a b)"), in_=psF[bk][:, 0:2 * T],
                                                                        func=AF.Tanh, scale=0.5),
                            r=(f"psF{bk}",), w=("tz",))
                    zs = zi * 2 + half
                    if zs == 0:
                        ret_b(0)
                    elif zs == 1:
                        ret_a(1)
                    elif zs == 2:
                        ret_b(1)
                    elif zs == 3:
                        transpose_tok(ytok[1], "ytok", yT[1], "yT1", 4)
                        dbg_store("yr", ytok[1][:], 512, "pool", t0)
                    elif zs == 4:
                        mem_attn()
                    else:
                        transpose_tok(ytok[2], "ytok", yT[2], "yT2", 4)
                        dbg_store("ym", ytok[2][:], 512, "pool", t0)

            mark(2)
            nblk = tt + 1
            items = [(h, b_) for h in range(8) for b_ in range(nblk)]

            def moba_s1(h, blk, idx):
                bb = h % 2
                if blk == 0:
                    if h == 0:
                        DMA("sp", "bt0", btab[0][:], bsk_b[0], r=("cv_bsk0",), w=("btab0",))
                    if h + 1 < 8:
                        DMA("sp", f"bt{(h + 1) % 2}", btab[(h + 1) % 2][:], bsk_b[h + 1], r=(f"cv_bsk{h + 1}",), w=(f"btab{(h + 1) % 2}",))
                use_mask = (blk < tt) and tt > 3
                bk = mmbank()
                for half, k0 in ((0, blk * 256 + 128), (1, blk * 256)):
                    PE(lambda e, half=half, k0=k0: e.matmul(psF[bk][:, half * T:(half + 1) * T], lhsT=kpair[h // 2][:, k0:k0 + 128], rhs=qaug[h][:],
                                                            start=True, stop=not use_mask, skip_group_check=True),
                       r=(f"kpair{h // 2}", f"qaug{h}"), w=(f"psF{bk}",))
                    if use_mask:
                        PE(lambda e, half=half: e.matmul(psF[bk][:, half * T:(half + 1) * T], lhsT=sel[:, blk, :], rhs=maskT[h][:],
                                                         start=False, stop=True, skip_group_check=True),
                           r=("sel", f"maskT{h}"), w=(f"psF{bk}",))
                pi = idx % 4
                dmin = t0 - (blk * 256 + 128)
                if dmin >= DELTA_FAR:
                    ACT(lambda e: e.activation(out=pT2[pi], in_=psF[bk][:, 0:2 * T], func=AF.Exp, bias=rb31[:, h:h + 1]),
                        r=(f"psF{bk}", "rb31"), w=(pT2n[pi],))
                else:
                    j0b = dmin + 128
                    si2 = idx % 2
                    DVE(lambda e: e.tensor_tensor(out=sb2[si2].rearrange("p (a b) -> p a b", a=2),
                                                  in0=psF[bk][:, 0:2 * T].rearrange("p (a b) -> p a b", a=2),
                                                  in1=_b(btab[bb][:, j0b:j0b + T], [[128, 2], [1, T]]), op=ALU.add),
                        r=(f"psF{bk}", f"btab{bb}"), w=(f"sb2_{si2}",))
                    ACT(lambda e: e.activation(out=pT2[pi], in_=sb2[si2], func=AF.Exp), r=(f"sb2_{si2}",), w=(pT2n[pi],))

            def moba_s2(h, blk, idx):
                acc = 4 + (h % 2)
                pi = idx % 4
                for half, c in ((0, 2 * blk + 1), (1, 2 * blk)):
                    for s_ in range(2):
                        PE(lambda e, s_=s_, half=half, c=c: e.matmul(psF[acc][:, s_ * 65:(s_ + 1) * 65],
                                                                     lhsT=pT2[pi][:, half * T + s_ * 128:half * T + (s_ + 1) * 128], rhs=Va[:, c, h, :],
                                                                     start=(blk == 0 and half == 0 and s_ == 0), stop=(blk == nblk - 1 and half == 1),
                                                                     skip_group_check=True),
                           r=(pT2n[pi], "Va"), w=(f"psF{acc}",))
                if blk == nblk - 1:
                    pov = psF[acc][:, 0:130].rearrange("p (s e) -> p s e", s=2)
                    DVE(lambda e: e.reciprocal(out=rden[:], in_=pov[:, :, 64]), r=(f"psF{acc}",), w=("rden",))
                    DVE(lambda e: e.tensor_tensor(out=ytok[0][:, :, h * 64:(h + 1) * 64], in0=pov[:, :, 0:64],
                                                  in1=_b(rden[:], [[1, 2], [0, 64]]), op=ALU.mult),
                        r=(f"psF{acc}", "rden"), w=("ytok",))

            def ya_pair_T(j):
                b = trbank()
                for s_ in range(2):
                    PE(lambda e, s_=s_: e.transpose(out=psB[b][:, s_ * 128:(s_ + 1) * 128], in_=ytok[0][:, s_, j * 128:(j + 1) * 128],
                                                    identity=ident[:]), r=("ytok", "ident"), w=(f"psB{b}",))
                DVE(lambda e: e.tensor_copy(out=yT[0][:, j, :], in_=psB[b][:, 0:T]), r=(f"psB{b}",), w=("yT0",))

            pend = []
            for i in range(len(items) + LA):
                if i < len(items):
                    moba_s1(items[i][0], items[i][1], i)
                if i >= LA:
                    hh, cc = items[i - LA]
                    moba_s2(hh, cc, i - LA)
                    if cc == nblk - 1 and hh % 2 == 1:
                        pend.append((i + max(2, nblk // 2), hh // 2))
                while pend and pend[0][0] <= i:
                    ya_pair_T(pend.pop(0)[1])
            for _, j in pend:
                ya_pair_T(j)
            dbg_store("ya", ytok[0][:], 512, "pool", t0)

            mark(5)
            if nxt is not None:
                DMA("sp", f"xld{nxt % 2}", Xq[:], x_d[nxt * T:(nxt + 1) * T, :].rearrange("(s p) d -> p s d", p=128), w=(Xqn,))
            wsl = [next_slab(hold=bi) for bi in range(3)]
            for fp in range(4):
                bks = []
                for bi in range(3):
                    bk = mmbank()
                    bks.append(bk)
                    for half in range(2):
                        c0 = (2 * fp + half) * 128
                        for kc in range(4):
                            PE(lambda e, bi=bi, bk=bk, half=half, c0=c0, kc=kc: e.matmul(psF[bk][:, half * T:(half + 1) * T], lhsT=wsl[bi][1][:, kc, c0:c0 + 128],
                                                                                      rhs=yT[bi][:, kc, :], start=(kc == 0), stop=(kc == 3), skip_group_check=True),
                               r=(wsl[bi][0], f"yT{bi}"), w=(f"psF{bk}",))
                tzv = [tz[:, bi, 2 * fp:2 * fp + 2, :].rearrange("p a b -> p (a b)") for bi in range(3)]
                DVE(lambda e, bk=bks[0], tzv=tzv: e.scalar_tensor_tensor(out=f32a[:], in0=tzv[0], scalar=1.0, in1=psF[bk][:, 0:2 * T],
                                                                         op0=ALU.add, op1=ALU.mult), r=("tz", f"psF{bks[0]}"), w=("f32a",))
                DVE(lambda e, bk=bks[1], tzv=tzv: e.scalar_tensor_tensor(out=f32b[:], in0=tzv[1], scalar=1.0, in1=psF[bk][:, 0:2 * T],
                                                                         op0=ALU.add, op1=ALU.mult), r=("tz", f"psF{bks[1]}"), w=("f32b",))
                POOL(lambda e: e.tensor_tensor(out=f32a[:], in0=f32a[:], in1=f32b[:], op=ALU.add), r=("f32a", "f32b"), w=("f32a",))
                DVE(lambda e, bk=bks[2], tzv=tzv: e.scalar_tensor_tensor(out=f32b[:], in0=tzv[2], scalar=1.0, in1=psF[bk][:, 0:2 * T],
                                                                         op0=ALU.add, op1=ALU.mult), r=("tz", f"psF{bks[2]}"), w=("f32b",))
                POOL(lambda e, fp=fp: e.tensor_tensor(out=mergedT[:, 2 * fp:2 * fp + 2, :].rearrange("p a b -> p (a b)"), in0=f32a[:], in1=f32b[:], op=ALU.add),
                     r=("f32a", "f32b"), w=("mergedT",))
            for ch in range(2):
                wn, wv = next_slab()
                for s_ in range(2):
                    bk = mmbank()
                    mm_tm(wn, wv, mergedT, "mergedT", s_, 8, bk)
                    DVE(lambda e, s_=s_, ch=ch, bk=bk: e.scalar_tensor_tensor(out=X[:, s_, ch * 512:(ch + 1) * 512], in0=psF[bk][:, 0:512], scalar=0.5,
                                                                              in1=X[:, s_, ch * 512:(ch + 1) * 512], op0=ALU.mult, op1=ALU.add),
                        r=(f"psF{bk}", Xn), w=(Xn,))

            dbg_store("h2", X[:], 1024, "sp", t0)
            mark(6)
            norm_to_T(Xn, X, gffn, "gffn", nT, "nT", T)
            for pg in range(6):
                npair = 4 if pg < 5 else 2
                wn_g, wv_g = next_slab()
                wn_u, wv_u = next_slab(hold=1)
                for pj in range(npair):
                    i = pg * 4 + pj
                    for which, (wn_, wv_) in enumerate(((wn_g, wv_g), (wn_u, wv_u))):
                        chn = i + 22 * which
                        bk = mmbank()
                        mm_fm(wn_, wv_, pj * 128, nT, "nT", 8, bk)
                        hb, ca = hbuf[which], cacc[which]
                        POOL(lambda e, hb=hb, chn=chn: e.tensor_copy(out=hb[:, 0:2], in_=carry[:, chn, :]), r=("carry",), w=(f"hbuf{which}",))
                        ACT(lambda e, hb=hb, bk=bk: e.activation(out=hb[:, 2:T + 2], in_=psF[bk][:, 0:T], func=AF.Identity),
                            r=(f"psF{bk}",), w=(f"hbuf{which}",))
                        ACT(lambda e, ca=ca, bk=bk, chn=chn: e.activation(out=ca[:], in_=psF[bk][:, 0:T], func=AF.Identity,
                                                                          scale=cw[:, chn, 2:3], bias=cb[:, chn:chn + 1]),
                            r=(f"psF{bk}", "cw", "cb"), w=(f"cacc{which}",))
                        DVE(lambda e, ca=ca, hb=hb, chn=chn: e.scalar_tensor_tensor(out=ca[:], in0=hb[:, 1:T + 1], scalar=cw[:, chn, 1:2], in1=ca[:],
                                                                                    op0=ALU.mult, op1=ALU.add),
                            r=(f"hbuf{which}", "cw", f"cacc{which}"), w=(f"cacc{which}",))
                        DVE(lambda e, ca=ca, hb=hb, chn=chn: e.scalar_tensor_tensor(out=ca[:], in0=hb[:, 0:T], scalar=cw[:, chn, 0:1], in1=ca[:],
                                                                                     op0=ALU.mult, op1=ALU.add),
                             r=(f"hbuf{which}", "cw", f"cacc{which}"), w=(f"cacc{which}",))
                        POOL(lambda e, hb=hb, chn=chn: e.tensor_copy(out=carry[:, chn, :], in_=hb[:, T:T + 2]), r=(f"hbuf{which}",), w=("carry",))
                    ACT(lambda e: e.activation(out=gact[:], in_=cacc[0][:], func=AF.Gelu), r=("cacc0",), w=("f32a",))
                    DVE(lambda e, i=i: e.tensor_tensor(out=aT[:, i, :], in0=gact[:], in1=cacc[1][:], op=ALU.mult), r=("f32a", "cacc1"), w=("aT",))
            if nxt is not None:
                norm_p1(Xqn, Xq, T)
            for ch in range(2):
                for kg in range(3):
                    nk = 8 if kg < 2 else 6
                    wn, wv = next_slab()
                    for s_ in range(2):
                        mm_tm(wn, wv, aT, "aT", s_, nk, 4 + s_, kofs=kg * 8, first=(kg == 0), last=(kg == 2))
                for s_ in range(2):
                    DVE(lambda e, s_=s_, ch=ch: e.tensor_tensor(out=X[:, s_, ch * 512:(ch + 1) * 512], in0=psF[4 + s_][:, 0:512],
                                                                in1=X[:, s_, ch * 512:(ch + 1) * 512], op=ALU.add),
                        r=(f"psF{4 + s_}", Xn), w=(Xn,))
            if nxt is not None:
                norm_p2(gmix, "gmix", nT, "nT", T)
            dbg_store("h3", X[:], 1024, "sp", t0)
            mute[0] = False
            DMA("sp", "gf0", f32a[:], AP(gfin_d.tensor, 0, [[0, 128], [1, 512]]), w=("f32a",))
            DMA("sp", "gf1", f32b[:], AP(gfin_d.tensor, 512, [[0, 128], [1, 512]]), w=("f32b",))
            rms_stats(Xn, X, 2, D)
            for s_ in range(2):
                DVE(lambda e, s_=s_: e.tensor_scalar(out=X[:, s_, :], in0=X[:, s_, :], scalar1=stat[:, 8 + s_:9 + s_], scalar2=None, op0=ALU.mult),
                    r=(Xn, "stat_r"), w=(Xn,))
                POOL(lambda e, s_=s_: e.tensor_tensor(out=X[:, s_, 0:512], in0=X[:, s_, 0:512], in1=f32a[:], op=ALU.mult), r=(Xn, "f32a"), w=(Xn,))
                POOL(lambda e, s_=s_: e.tensor_tensor(out=X[:, s_, 512:1024], in0=X[:, s_, 512:1024], in1=f32b[:], op=ALU.mult), r=(Xn, "f32b"), w=(Xn,))
            DMA("sp", f"st{tt % 2}", y_d[t0:t0 + T, :].rearrange("(s p) d -> p s d", p=128), X[:], r=(Xn,), w=("ydram",))

        P.emit("sp", None, [K("ydram"), K("xs0"), K("xs1"), K("dbgdram")], [K("xs0"), K("xs1"), K("ytok")])

        def semfor(key, val):
            kind, nm = key
            if kind == "dma":
                return dsem[nm], val
            ep = (val - 1) // EPOCH
            return esem[nm][ep], val - ep * EPOCH

        block = es.enter_context(nc.Block())

        def replay(eng_name, eng):
            for waits, fn, tok in P.q[eng_name]:
                emb = None
                if fn is not None and waits and EMBED_WAIT and eng_name != "pe":
                    emb = waits[-1]
                    waits = waits[:-1]
                for k, v in waits:
                    s_h, s_v = semfor(k, v)
                    eng.wait_ge(s_h, s_v)
                if fn is None:
                    continue
                ins = fn(eng)
                if emb is not None:
                    s_h, s_v = semfor(emb[0], emb[1])
                    ins._wait_ge(s_h, s_v)
                s_h, _ = semfor(tok[0], tok[1])
                ins.then_inc(s_h, 16 if tok[0][0] == "dma" else 1)

        for e_ in Prog.ENG:
            assert (P.cnt[e_] + EPOCH - 1) // EPOCH <= nep[e_], (e_, P.cnt[e_])

        @block.sync
        def _(e):
            replay("sp", e)

        @block.tensor
        def _(e):
            replay("pe", e)

        @block.scalar
        def _(e):
            replay("act", e)

        @block.vector
        def _(e):
            replay("dve", e)

        @block.gpsimd
        def _(e):
            replay("pool", e)

    return nc, P


def _prep_inputs(inputs):
    f = lambda a: np.ascontiguousarray(np.asarray(a, dtype=np.float32))
    col = lambda g: np.ascontiguousarray(f(g).reshape(8, 128).T)
    rel_bias = f(inputs["rel_bias"])
    kk = np.arange(128)[:, None]
    jj = np.arange(TABL)[None, :]
    d = jj - 128 - kk
    bidx = _rel_bucket_np(d)
    bsk = np.empty((8, 128, TABL), np.float32)
    for h in range(8):
        bsk[h] = np.where(d >= 0, rel_bias[bidx, h], np.float32(NEG))
    conv_w = f(inputs["conv_w"])[0]
    conv_b = f(inputs["conv_b"])[0]
    cw = np.ascontiguousarray(conv_w.reshape(3, 44, 128).transpose(2, 1, 0)).reshape(128, 44 * 3)
    cb = np.ascontiguousarray(conv_b.reshape(44, 128).T)
    shared = {
        "w_in": f(inputs["w_in"])[0], "w_mem_kv": f(inputs["w_mem_kv"])[0],
        "w_br_attn": f(inputs["w_br_attn"])[0], "w_br_ret": f(inputs["w_br_ret"])[0], "w_br_mem": f(inputs["w_br_mem"])[0],
        "w_out": f(inputs["w_out"])[0], "w_up": f(inputs["w_up"])[0], "w_down": f(inputs["w_down"])[0],
        "gmix_col": col(inputs["g_mix"]), "gffn_col": col(inputs["g_ffn"]), "gmem_col": col(inputs["g_mem"]),
        "g_final": f(inputs["g_final"]).reshape(1, D), "ret_gn_gain": f(inputs["ret_gn_gain"]).reshape(1, 512),
        "cw_col": cw, "cb_col": cb, "rb31": np.ascontiguousarray(rel_bias[31:32, :]), "bias_skew": bsk,
        "c_ident": _HC["ident"], "c_mask01T": _HC["mask01T"], "c_sel": _HC["sel"], "c_candmask": _HC["candmask"],
        "c_rt_cos": _HC["rt_cos"], "c_rt_sin": _HC["rt_sin"], "c_scl": _HC["scl"],
    }
    x = f(inputs["x"])
    mem = f(inputs["mem"])
    maps = []
    for b in range(NB):
        m = dict(shared)
        m["x"] = x[b]
        m["mem"] = mem[b]
        maps.append(m)
    return maps


def kernel(**inputs):
    maps = _prep_inputs(inputs)
    nc, _ = build_program()
    res = run_bass_kernel_spmd(nc, maps, core_ids=list(range(NB)))
    return np.stack([np.asarray(r["y"], dtype=np.float32) for r in res.results], axis=0)
```
